# Optimizing a Trainium2 kernel written in Bass

```python
import jax, jax.numpy as jnp
from jax import lax
import numpy as np

D_MODEL = 1024
BATCH = 8
SEQ = 4096
DEPTH = 4

GRID_W = 64
CTX_LEN = 256

HEAD_DIM = 64
N_Q_HEADS = 8
N_KV_HEADS = 2
GQA_GROUP = N_Q_HEADS // N_KV_HEADS
WINDOW = 128
ATT_BLOCK = 128
ROPE_THETA = 10000.0
Q_W = N_Q_HEADS * HEAD_DIM
KV_W = N_KV_HEADS * HEAD_DIM

CHUNK = 128
SGU_GROUPS = 4
SGU_WIDTH = 512
SGU_GROUP_W = SGU_WIDTH // SGU_GROUPS

FNET_GROUPS = 4
FNET_WIDTH = 512
FNET_GROUP_W = FNET_WIDTH // FNET_GROUPS

BRANCH_W = 512
N_BRANCH = 3
IN_W = Q_W + 2 * KV_W + 2 * SGU_WIDTH + FNET_WIDTH
IN_SPLITS = (Q_W, Q_W + KV_W, Q_W + 2 * KV_W, Q_W + 2 * KV_W + SGU_WIDTH, Q_W + 2 * KV_W + 2 * SGU_WIDTH)

D_FF = 2752
N_MOD = 9
ALPHA = (2 * DEPTH) ** 0.25
BETA = (8 * DEPTH) ** -0.25
LN_EPS = 1e-5
NEG_INF = -1e30

kernel_name = 'hybrid_diffusion_gated_mixers'


def layer_norm(x, g, b):
    xf = x.astype(jnp.float32)
    mu = jnp.mean(xf, axis=-1, keepdims=True)
    var = jnp.mean(jnp.square(xf - mu), axis=-1, keepdims=True)
    return ((xf - mu) * lax.rsqrt(var + LN_EPS)).astype(x.dtype) * g + b


def modulate(x, shift, scale):
    return x * (1 + scale) + shift


def swiglu(h, w_up, w_down):
    gate, up = jnp.split(h @ w_up, 2, axis=-1)
    return (jax.nn.silu(gate) * up) @ w_down


def axial_rope_tables(n_tokens, dtype):
    rows = n_tokens // GRID_W
    row = jnp.broadcast_to(jnp.arange(rows)[:, None], (rows, GRID_W)).reshape(-1).astype(jnp.float32)
    col = jnp.broadcast_to(jnp.arange(GRID_W)[None, :], (rows, GRID_W)).reshape(-1).astype(jnp.float32)
    axis_dim = HEAD_DIM // 2
    inv_freq = ROPE_THETA ** (-jnp.arange(0, axis_dim, 2, dtype=jnp.float32) / axis_dim)
    ang_r = row[:, None] * inv_freq[None, :]
    ang_c = col[:, None] * inv_freq[None, :]
    return (jnp.cos(ang_r).astype(dtype), jnp.sin(ang_r).astype(dtype),
            jnp.cos(ang_c).astype(dtype), jnp.sin(ang_c).astype(dtype))


def _rotate(x, cos, sin):
    x1, x2 = jnp.split(x, 2, axis=-1)
    cos, sin = cos[:, None, :], sin[:, None, :]
    return jnp.concatenate([x1 * cos - x2 * sin, x1 * sin + x2 * cos], axis=-1)


def apply_axial_rope(x, tables):
    cos_r, sin_r, cos_c, sin_c = tables
    xr, xc = jnp.split(x, 2, axis=-1)
    return jnp.concatenate([_rotate(xr, cos_r, sin_r), _rotate(xc, cos_c, sin_c)], axis=-1)


def windowed_gqa_with_context(q_lat, k_lat, v_lat, k_ctx, v_ctx, sink):
    b, s = q_lat.shape[:2]
    n_ctx = k_ctx.shape[1]
    nb = s // ATT_BLOCK
    band_len = 3 * ATT_BLOCK
    scale = HEAD_DIM ** -0.5
    qb = q_lat.reshape(b, nb, ATT_BLOCK, N_KV_HEADS, GQA_GROUP, HEAD_DIM)

    def band(t):
        tp = jnp.pad(t, ((0, 0), (ATT_BLOCK, ATT_BLOCK), (0, 0), (0, 0)))
        tp = tp.reshape(b, nb + 2, ATT_BLOCK, N_KV_HEADS, HEAD_DIM)
        return jnp.concatenate([tp[:, :-2], tp[:, 1:-1], tp[:, 2:]], axis=2)

    kb, vb = band(k_lat), band(v_lat)
    q_pos = jnp.arange(s).reshape(nb, ATT_BLOCK)
    k_pos = (jnp.arange(nb)[:, None] - 1) * ATT_BLOCK + jnp.arange(band_len)[None, :]
    valid = ((jnp.abs(q_pos[:, :, None] - k_pos[:, None, :]) <= WINDOW)
             & (k_pos[:, None, :] >= 0) & (k_pos[:, None, :] < s))
    s_band = jnp.einsum('bnqkgd,bnjkd->bnkgqj', qb, kb).astype(jnp.float32) * scale
    s_band = jnp.where(valid[None, :, None, None], s_band, NEG_INF)
    s_ctx = jnp.einsum('bnqkgd,bjkd->bnkgqj', qb, k_ctx).astype(jnp.float32) * scale
    s_sink = jnp.broadcast_to(sink.astype(jnp.float32).reshape(1, 1, N_KV_HEADS, GQA_GROUP, 1, 1),
                              s_band.shape[:-1] + (1,))
    p = jax.nn.softmax(jnp.concatenate([s_band, s_ctx, s_sink], axis=-1), axis=-1).astype(v_lat.dtype)
    o = (jnp.einsum('bnkgqj,bnjkd->bnqkgd', p[..., :band_len], vb)
         + jnp.einsum('bnkgqj,bjkd->bnqkgd', p[..., band_len:band_len + n_ctx], v_ctx))
    return o.reshape(b, s, Q_W)


def context_gqa_with_sink(q_ctx, k_ctx, v_ctx, sink):
    b, n_ctx = q_ctx.shape[:2]
    qg = q_ctx.reshape(b, n_ctx, N_KV_HEADS, GQA_GROUP, HEAD_DIM)
    sc = jnp.einsum('bqkgd,bjkd->bkgqj', qg, k_ctx).astype(jnp.float32) * HEAD_DIM ** -0.5
    s_sink = jnp.broadcast_to(sink.astype(jnp.float32).reshape(1, N_KV_HEADS, GQA_GROUP, 1, 1),
                              sc.shape[:-1] + (1,))
    p = jax.nn.softmax(jnp.concatenate([sc, s_sink], axis=-1), axis=-1)[..., :n_ctx].astype(v_ctx.dtype)
    return jnp.einsum('bkgqj,bjkd->bqkgd', p, v_ctx).reshape(b, n_ctx, Q_W)


def chunked_sgu(u, v, w_s, b_s, g, bn):
    b, n = u.shape[:2]
    u = jax.nn.gelu(u)
    v = layer_norm(jax.nn.gelu(v), g, bn)
    vb = v.reshape(b, n // CHUNK, CHUNK, SGU_GROUPS, SGU_GROUP_W)
    mixed = jnp.einsum('gpq,bnqgc->bnpgc', w_s, vb) + b_s.T[None, None, :, :, None]
    return u * mixed.reshape(b, n, SGU_WIDTH)


def fourier_mix(f):
    b, n = f.shape[:2]
    fg = f.reshape(b, n, FNET_GROUPS, FNET_GROUP_W).astype(jnp.float32)
    y = jnp.fft.fftn(fg, axes=(1, 3), norm='ortho').real
    return y.astype(f.dtype).reshape(b, n, FNET_WIDTH)


def gated_merge(h, branches, w_gate, w_branch, w_out):
    merged = jax.nn.sigmoid(h @ w_gate[0]) * (branches[0] @ w_branch[0])
    for r in range(1, N_BRANCH):
        merged = merged + jax.nn.sigmoid(h @ w_gate[r]) * (branches[r] @ w_branch[r])
    return merged @ w_out


def mixer_layer(h_lat, h_ctx, w_in, attn_sink, sgu_w, sgu_b, sgu_ln_g, sgu_ln_b, w_gate, w_branch, w_out, need_ctx):
    b, s = h_lat.shape[:2]
    n_ctx = h_ctx.shape[1]
    q_l, k_l, v_l, u_l, z_l, f_l = jnp.split(h_lat @ w_in, IN_SPLITS, axis=-1)
    tables = axial_rope_tables(s, h_lat.dtype)
    q_l = apply_axial_rope(q_l.reshape(b, s, N_Q_HEADS, HEAD_DIM), tables)
    k_l = apply_axial_rope(k_l.reshape(b, s, N_KV_HEADS, HEAD_DIM), tables)
    v_l = v_l.reshape(b, s, N_KV_HEADS, HEAD_DIM)
    if need_ctx:
        q_c, k_c, v_c, u_c, z_c, f_c = jnp.split(h_ctx @ w_in, IN_SPLITS, axis=-1)
    else:
        k_c, v_c = jnp.split(h_ctx @ w_in[:, Q_W:Q_W + 2 * KV_W], 2, axis=-1)
    k_c = k_c.reshape(b, n_ctx, N_KV_HEADS, HEAD_DIM)
    v_c = v_c.reshape(b, n_ctx, N_KV_HEADS, HEAD_DIM)
    branches_lat = (windowed_gqa_with_context(q_l, k_l, v_l, k_c, v_c, attn_sink),
                    chunked_sgu(u_l, z_l, sgu_w, sgu_b, sgu_ln_g, sgu_ln_b),
                    fourier_mix(f_l))
    y_lat = gated_merge(h_lat, branches_lat, w_gate, w_branch, w_out)
    y_ctx = None
    if need_ctx:
        branches_ctx = (context_gqa_with_sink(q_c, k_c, v_c, attn_sink),
                        chunked_sgu(u_c, z_c, sgu_w, sgu_b, sgu_ln_g, sgu_ln_b),
                        fourier_mix(f_c))
        y_ctx = gated_merge(h_ctx, branches_ctx, w_gate, w_branch, w_out)
    return y_lat, y_ctx


def ffn_sublayer(x, mods, j, w_up, w_down, g, b):
    sub = swiglu(modulate(x, mods[3 * j], mods[3 * j + 1]), w_up, w_down)
    return layer_norm(ALPHA * x + 0.5 * mods[3 * j + 2] * sub, g, b)


def setup_inputs(seed: int = 0) -> dict:
    key = jax.random.key(seed)
    ks = jax.random.split(key, 19)
    nrm = lambda k, shape, s: jax.random.normal(k, shape, jnp.float32) * s
    return {
        'x': nrm(ks[0], (BATCH, SEQ, D_MODEL), 1.0),
        'c': nrm(ks[1], (BATCH, D_MODEL), 1.0),
        'ctx': nrm(ks[2], (BATCH, CTX_LEN, D_MODEL), 1.0),
        'c_ctx': nrm(ks[3], (D_MODEL,), 1.0),
        'w_mod': nrm(ks[4], (DEPTH, D_MODEL, N_MOD * D_MODEL), D_MODEL ** -0.5),
        'b_mod': nrm(ks[5], (DEPTH, N_MOD * D_MODEL), 0.02),
        'w_ffn_up': nrm(ks[6], (DEPTH, 2, D_MODEL, 2 * D_FF), D_MODEL ** -0.5),
        'w_ffn_down': nrm(ks[7], (DEPTH, 2, D_FF, D_MODEL), BETA * D_FF ** -0.5),
        'ln_g': 1.0 + nrm(ks[8], (DEPTH, 3, D_MODEL), 0.02),
        'ln_b': nrm(ks[9], (DEPTH, 3, D_MODEL), 0.02),
        'w_in': nrm(ks[10], (DEPTH, D_MODEL, IN_W), D_MODEL ** -0.5),
        'attn_sink': nrm(ks[11], (DEPTH, N_Q_HEADS), 0.5),
        'sgu_w': nrm(ks[12], (DEPTH, SGU_GROUPS, CHUNK, CHUNK), CHUNK ** -0.5),
        'sgu_b': 1.0 + nrm(ks[13], (DEPTH, SGU_GROUPS, CHUNK), 0.02),
        'sgu_ln_g': 1.0 + nrm(ks[14], (DEPTH, SGU_WIDTH), 0.02),
        'sgu_ln_b': nrm(ks[15], (DEPTH, SGU_WIDTH), 0.02),
        'w_gate': nrm(ks[16], (DEPTH, N_BRANCH, D_MODEL, D_MODEL), D_MODEL ** -0.5),
        'w_branch': nrm(ks[17], (DEPTH, N_BRANCH, BRANCH_W, D_MODEL), BRANCH_W ** -0.5),
        'w_out': nrm(ks[18], (DEPTH, D_MODEL, D_MODEL), BETA * D_MODEL ** -0.5),
    }


def reference(x, c, ctx, c_ctx, w_mod, b_mod, w_ffn_up, w_ffn_down, ln_g, ln_b, w_in, attn_sink,
              sgu_w, sgu_b, sgu_ln_g, sgu_ln_b, w_gate, w_branch, w_out):
    x_lat, x_ctx = x, ctx
    for i in range(DEPTH):
        last = i == DEPTH - 1
        m_lat = jnp.split((jax.nn.silu(c) @ w_mod[i] + b_mod[i])[:, None, :], N_MOD, axis=-1)
        m_ctx = jnp.split(jax.nn.silu(c_ctx) @ w_mod[i] + b_mod[i], N_MOD, axis=-1)
        x_lat = ffn_sublayer(x_lat, m_lat, 0, w_ffn_up[i, 0], w_ffn_down[i, 0], ln_g[i, 0], ln_b[i, 0])
        x_ctx = ffn_sublayer(x_ctx, m_ctx, 0, w_ffn_up[i, 0], w_ffn_down[i, 0], ln_g[i, 0], ln_b[i, 0])
        y_lat, y_ctx = mixer_layer(modulate(x_lat, m_lat[3], m_lat[4]), modulate(x_ctx, m_ctx[3], m_ctx[4]),
                                   w_in[i], attn_sink[i], sgu_w[i], sgu_b[i], sgu_ln_g[i], sgu_ln_b[i],
                                   w_gate[i], w_branch[i], w_out[i], not last)
        x_lat = layer_norm(ALPHA * x_lat + m_lat[5] * y_lat, ln_g[i, 1], ln_b[i, 1])
        x_lat = ffn_sublayer(x_lat, m_lat, 2, w_ffn_up[i, 1], w_ffn_down[i, 1], ln_g[i, 2], ln_b[i, 2])
        if not last:
            x_ctx = layer_norm(ALPHA * x_ctx + m_ctx[5] * y_ctx, ln_g[i, 1], ln_b[i, 1])
            x_ctx = ffn_sublayer(x_ctx, m_ctx, 2, w_ffn_up[i, 1], w_ffn_down[i, 1], ln_g[i, 2], ln_b[i, 2])
    return x_lat
```

```python
import contextlib
import math
import numpy as np
import ml_dtypes
import concourse.bass as bass
import concourse.mybir as mybir
from concourse.bass_utils import run_bass_kernel_spmd

F32 = mybir.dt.float32
BF16 = mybir.dt.bfloat16
U8 = mybir.dt.uint8
AF = mybir.ActivationFunctionType
ALU = mybir.AluOpType

D = 1024
SEQ = 4096
CTX = 256
T = SEQ + CTX
NT = T // 128
DEPTH = 4
DFF = 2752
NFC = 22
IN_W = 2304
ALPHA = (2 * DEPTH) ** 0.25
LN_EPS = 1e-5
GELU_C = 0.7978845608028654

ENGS = ("pe", "act", "dve", "pool", "sp")
SAME_ENGINE_SYNC = True


class Buf:
    __slots__ = ("name", "w", "r")

    def __init__(self, name=""):
        self.name = name
        self.w = []
        self.r = []


class Prog:
    N_DMA_SEMS = 40
    N_SW_SEMS = 8

    def __init__(self, nc):
        self.nc = nc
        self.ops = {e: [] for e in ENGS}
        self.cnt = {e: 0 for e in ENGS}
        self.seen = {e: {} for e in ENGS}
        self.dma_cnt = [0] * self.N_DMA_SEMS
        self.dma_next = 0
        self.sw_next = 0
        self.n_instr = 0

    def _collect(self, e, reads, writes, extra=(), is_dma=False):
        waits = {}

        def add(ev):
            k, v = ev
            if k == e and (e == "pe" or not SAME_ENGINE_SYNC):
                return
            if waits.get(k, 0) < v:
                waits[k] = v

        for b in reads:
            for ev in b.w:
                add(ev)
        for b in writes:
            for ev in b.w:
                add(ev)
            for ev in b.r:
                if ev[0] == e and not is_dma:
                    continue
                add(ev)
        for ev in extra:
            add(ev)
        seen = self.seen[e]
        for k, v in waits.items():
            if seen.get(k, 0) < v:
                seen[k] = v
                self.ops[e].append(("wait", k, v))

    @staticmethod
    def _compress(evs):
        d = {}
        for k, v in evs:
            if d.get(k, 0) < v:
                d[k] = v
        return list(d.items())

    def _record(self, ev, reads, writes):
        for b in reads:
            b.r.append(ev)
            if len(b.r) > 48:
                b.r = self._compress(b.r)
        for b in writes:
            b.w = [ev]
            b.r = []

    def op(self, e, fn, reads=(), writes=()):
        self._collect(e, reads, writes)
        self.cnt[e] += 1
        ev = (e, self.cnt[e])
        self.ops[e].append(("op", fn))
        self._record(ev, reads, writes)
        self.n_instr += 1
        return ev

    def mm(self, fn, reads, psum, first):
        self._collect("pe", reads, [psum] if first else [])
        self.cnt["pe"] += 1
        ev = ("pe", self.cnt["pe"])
        self.ops["pe"].append(("op", fn))
        for b in reads:
            b.r.append(ev)
            if len(b.r) > 48:
                b.r = self._compress(b.r)
        if first:
            psum.r = []
        psum.w = [ev]
        self.n_instr += 1
        return ev

    def dma(self, e, fn, reads=(), writes=()):
        if e == "pool":
            j = self.N_DMA_SEMS - self.N_SW_SEMS + self.sw_next
            self.sw_next = (self.sw_next + 1) % self.N_SW_SEMS
        else:
            j = self.dma_next
            self.dma_next = (self.dma_next + 1) % (self.N_DMA_SEMS - self.N_SW_SEMS)
        key = ("dma", j)
        prev = self.dma_cnt[j]
        extra = [(key, prev)] if prev else []
        self._collect(e, reads, writes, extra, is_dma=True)
        self.dma_cnt[j] += 16
        ev = (key, self.dma_cnt[j])
        self.ops[e].append(("dma", fn, key))
        self._record(ev, reads, writes)
        self.n_instr += 1
        return ev

    def gate_swdge(self, e="dve"):
        seen = self.seen[e]
        for j in range(self.N_DMA_SEMS - self.N_SW_SEMS, self.N_DMA_SEMS):
            k, v = ("dma", j), self.dma_cnt[j]
            if v and seen.get(k, 0) < v:
                seen[k] = v
                self.ops[e].append(("wait", k, v))

    def _all_events(self):
        evs = [(e, self.cnt[e]) for e in ENGS if self.cnt[e]]
        evs += [(("dma", j), c) for j, c in enumerate(self.dma_cnt) if c]
        return evs

    def barrier(self):
        evs = self._all_events()
        for e in ENGS:
            seen = self.seen[e]
            for k, v in evs:
                if k == e and e == "pe":
                    continue
                if seen.get(k, 0) < v:
                    seen[k] = v
                    self.ops[e].append(("wait", k, v))

    def final_wait(self, e="sp"):
        seen = self.seen[e]
        for k, v in self._all_events():
            if seen.get(k, 0) < v:
                seen[k] = v
                self.ops[e].append(("wait", k, v))

    def emit(self):
        nc = self.nc
        with contextlib.ExitStack() as st:
            sems = {}
            for e in ENGS:
                sems[e] = st.enter_context(nc.semaphore("s_" + e))
            for j in range(self.N_DMA_SEMS):
                sems[("dma", j)] = st.enter_context(nc.semaphore("s_dma%d" % j))
            block = st.enter_context(nc.Block())

            def run(eng, e):
                mysem = sems[e]
                for item in self.ops[e]:
                    t = item[0]
                    if t == "wait":
                        eng.wait_ge(sems[item[1]], item[2])
                    elif t == "op":
                        item[1](eng).then_inc(mysem, 1)
                    else:
                        item[1](eng).then_inc(sems[item[2]], 16)

            @block.tensor
            def _(eng):
                run(eng, "pe")

            @block.scalar
            def _(eng):
                run(eng, "act")

            @block.vector
            def _(eng):
                run(eng, "dve")

            @block.gpsimd
            def _(eng):
                run(eng, "pool")

            @block.sync
            def _(eng):
                run(eng, "sp")


def _dtsize(dt):
    return 4 if dt == F32 else 2


class Arena:
    def __init__(self, ap_u8, size):
        self.ap = ap_u8
        self.size = size
        self.off = 0

    def reset(self):
        self.off = 0

    def alloc(self, shape, dt):
        n = 1
        for s in shape[1:]:
            n *= s
        nbytes = n * _dtsize(dt)
        off = self.off
        self.off += (nbytes + 63) // 64 * 64
        assert self.off <= self.size, "arena overflow %d > %d" % (self.off, self.size)
        a = self.ap[0:shape[0], off:off + nbytes].bitcast(dt)
        if len(shape) == 3:
            a = a.rearrange("p (a b) -> p a b", a=shape[1])
        elif len(shape) == 4:
            a = a.rearrange("p (a b c) -> p a b c", a=shape[1], b=shape[2])
        elif len(shape) == 5:
            a = a.rearrange("p (a b c d) -> p a b c d", a=shape[1], b=shape[2], c=shape[3])
        return a


def _bf(a):
    return np.asarray(a, dtype=np.float32).astype(ml_dtypes.bfloat16)


def make_consts():
    c = {}
    c["ident_f"] = np.eye(128, dtype=np.float32)
    c["ident_b"] = _bf(np.eye(128))
    c["ones_b"] = _bf(np.ones((128, 64)))
    c["ones_f"] = np.ones((1, 128), dtype=np.float32)
    tok = np.arange(SEQ)
    row = (tok // 64).astype(np.float32)
    col = (tok % 64).astype(np.float32)
    inv_freq = (np.float32(10000.0) ** (-np.arange(0, 32, 2, dtype=np.float32) / np.float32(32))).astype(np.float32)
    cosT = np.zeros((128, SEQ), np.float32)
    sinT = np.zeros((128, SEQ), np.float32)
    for p in range(128):
        d = p % 64
        pos = row if d < 32 else col
        fi = (d % 32) % 16
        ang = (pos * inv_freq[fi]).astype(np.float32)
        cosT[p] = np.cos(ang)
        s = np.sin(ang)
        sinT[p] = -s if (d % 32) < 16 else s
    c["ropec"] = cosT
    c["ropes"] = sinT
    j = np.arange(128)[:, None]
    i = np.arange(128)[None, :]
    mp = np.where(j < i, -30000.0, 0.0)
    mn = np.where(j > i, -30000.0, 0.0)
    c["maskp"] = _bf(np.tile(mp, (1, 4)))
    c["maskn"] = _bf(np.tile(mn, (1, 4)))
    n = np.arange(128, dtype=np.float64)
    a128 = 2 * np.pi * np.outer(n, n) / 128.0
    c["c128"] = _bf(np.cos(a128) / np.sqrt(128.0))
    c["s128n"] = _bf(-np.sin(a128) / np.sqrt(128.0))
    c["s128"] = _bf(np.sin(a128) / np.sqrt(128.0))
    k1 = np.arange(128, dtype=np.float64)[:, None]
    n2 = np.arange(32, dtype=np.float64)[None, :]
    atw = 2 * np.pi * k1 * n2 / 4096.0
    c["tw"] = np.concatenate([np.cos(atw), np.sin(atw)], axis=1).astype(np.float32)
    w2 = np.zeros((128, 128), np.float64)
    for k1lo in range(2):
        for ri in range(2):
            for nn in range(32):
                r_ = k1lo * 64 + ri * 32 + nn
                for rix in range(2):
                    for k2 in range(32):
                        c_ = k1lo * 64 + rix * 32 + k2
                        ang = 2 * np.pi * k2 * nn / 32.0
                        cs, sn = np.cos(ang), np.sin(ang)
                        if ri == 0:
                            v = cs if rix == 0 else -sn
                        else:
                            v = sn if rix == 0 else cs
                        w2[r_, c_] = v / np.sqrt(32.0)
    c["w2"] = _bf(w2)
    c["ccscn"] = _bf(np.concatenate([np.cos(a128), -np.sin(a128)], axis=1) / np.sqrt(128.0))
    nn = np.arange(256, dtype=np.float64)
    a256 = 2 * np.pi * np.outer(nn, nn) / 256.0
    c["c256"] = _bf((np.cos(a256) / 16.0).reshape(2, 128, 256).transpose(1, 0, 2))
    c["s256"] = _bf((np.sin(a256) / 16.0).reshape(2, 128, 256).transpose(1, 0, 2))
    return c


CONST_SPECS = {
    "ident_f": ([128, 128], F32), "ident_b": ([128, 128], BF16), "ones_b": ([128, 64], BF16), "ones_f": ([1, 128], F32),
    "ropec": ([128, SEQ], F32), "ropes": ([128, SEQ], F32), "maskp": ([128, 512], BF16), "maskn": ([128, 512], BF16),
    "c128": ([128, 128], BF16), "s128n": ([128, 128], BF16), "s128": ([128, 128], BF16), "tw": ([128, 64], F32),
    "w2": ([128, 128], BF16), "ccscn": ([128, 256], BF16), "c256": ([128, 2, 256], BF16), "s256": ([128, 2, 256], BF16),
}

WEIGHT_SPECS = {
    "w_mod": [D, 9 * D], "b_mod": [9 * D], "w_ffn_up": [2, D, 2 * DFF], "w_ffn_down": [2, DFF, D],
    "ln_g": [3, D], "ln_b": [3, D], "w_in": [D, IN_W], "attn_sink": [8], "sgu_w": [4, 128, 128], "sgu_b": [4, 128],
    "sgu_ln_g": [512], "sgu_ln_b": [512], "w_gate": [3, D, D], "w_branch": [3, 512, D], "w_out": [D, D],
}


class Builder:
    def __init__(self, L=DEPTH, stop_after=None, debug_out=(), wl=None):
        self.L = L
        wl = wl or L
        self.wl = wl
        self.stop_after = stop_after
        nc = self.nc = bass.Bass("TRN2", target_bir_lowering=False)
        self.P = Prog(nc)
        P = self.P
        dt = nc.dram_tensor
        self.x_in = dt("x", [SEQ, D], F32, kind="ExternalInput").ap()
        self.c_in = dt("c", [1, D], F32, kind="ExternalInput").ap()
        self.ctx_in = dt("ctx", [CTX, D], F32, kind="ExternalInput").ap()
        self.cctx_in = dt("c_ctx", [1, D], F32, kind="ExternalInput").ap()
        self.W = {}
        for k, shp in WEIGHT_SPECS.items():
            self.W[k] = dt(k, [wl] + shp, F32, kind="ExternalInput").ap()
        self.C = {}
        for k, (shp, d_) in CONST_SPECS.items():
            self.C[k] = dt(k, shp, d_, kind="ExternalInput").ap()
        self.out = dt("out", [SEQ, D], F32, kind="ExternalOutput").ap()
        self.x_d = dt("x_d", [T, D], F32).ap()
        self.mods_d = dt("mods_d", [L, 2, 9 * D], F32).ap()
        self.aT_d = dt("aT_d", [NFC, 128, T], BF16).ap()
        self.hT_d = dt("hT_d", [8, 128, T], BF16).ap()
        self.qT_d = dt("qT_d", [8, 64, T], BF16).ap()
        self.kT_d = dt("kT_d", [2, 64, T], BF16).ap()
        self.v_d = dt("v_d", [T, 128], BF16).ap()
        self.f_d = dt("f_d", [SEQ, 512], BF16).ap()
        self.fcT_d = dt("fcT_d", [4, 128, CTX], BF16).ap()
        self.brT_d = [dt("attT_d", [8, 64, T], BF16).ap(), dt("sguT_d", [4, 128, T], BF16).ap(), dt("fouT_d", [4, 128, T], BF16).ap()]
        self.B_d = dt("B_d", [128, 2, 32, 512], BF16).ap()
        self.dbg = {}
        for name, shp, d_, getter in debug_out:
            self.dbg[name] = (dt(name, shp, d_, kind="ExternalOutput").ap(), getter)
        self.b_xsrc = [Buf("xin%d" % t) for t in range(NT)]
        self.b_xd = [Buf("xd%d" % t) for t in range(NT)]
        self.b_mods = [Buf("mods_d%d" % i) for i in range(L)]
        self.b_aT = [Buf("aT_d%d" % g) for g in range(9)]
        self.b_hT = [Buf("hT_d%d" % g) for g in range(9)]
        self.b_q = [Buf() for g in range(9)]
        self.b_k = [Buf() for g in range(9)]
        self.b_v = [Buf() for g in range(9)]
        self.b_f = [Buf() for g in range(8)]
        self.b_fc = Buf()
        self.b_br = [[Buf() for g in range(9)] for r in range(3)]
        self.b_Bd = [Buf() for g in range(8)]
        self.b_const = Buf("const")
        self.b_w = Buf("weights")

        sb = nc.alloc_sbuf_tensor
        self.ident_f = sb("sb_ident_f", [128, 128], F32).ap()
        self.ident_b = sb("sb_ident_b", [128, 128], BF16).ap()
        self.ones_b = sb("sb_ones_b", [128, 64], BF16).ap()
        self.ones_f = sb("sb_ones_f", [1, 128], F32).ap()
        self.modT = sb("modT", [128, L, 2, 72], F32).ap()
        self.eps_t = sb("eps_t", [128, 1], F32).ap()
        self.b_pc = Buf("persist_consts")
        self.b_modT = [Buf("modT%d" % i) for i in range(L)]
        self.sT = sb("sT", [128, 8, 2], F32).ap()
        self.b_sT = Buf("sT")
        rem = nc.sbuf_bytes_remaining - 256
        self.arena_size = rem // 64 * 64
        self.arena = Arena(sb("arena", [128, self.arena_size], U8).ap(), self.arena_size)
        self.psum = [nc.alloc_psum_tensor("ps%d" % i, [128, 512], F32).ap() for i in range(8)]
        self.b_ps = [Buf("ps%d" % i) for i in range(8)]
        self.ps_next = 0

        for nm, dst in (("ident_f", self.ident_f), ("ident_b", self.ident_b), ("ones_b", self.ones_b), ("ones_f", self.ones_f)):
            self.DMA("sp", dst, self.C[nm], [self.b_const], [self.b_pc])
        P.op("dve", lambda e: e.memset(self.eps_t, LN_EPS), writes=[self.b_pc])

    def ps(self):
        i = self.ps_next
        self.ps_next = (i + 1) % 8
        return self.psum[i], self.b_ps[i]

    def DMA(self, q, out, in_, reads, writes, **kw):
        self.P.dma(q, lambda e: e.dma_start(out=out, in_=in_, **kw), reads, writes)

    def MM(self, out, lhsT, rhs, start, stop, reads, psb):
        self.P.mm(lambda e: e.matmul(out, lhsT=lhsT, rhs=rhs, start=start, stop=stop), reads, psb, first=start)

    def TR(self, out, in_, ident, reads, psb, first):
        self.P.mm(lambda e: e.transpose(out, in_, ident), reads, psb, first=first)

    def ACT(self, out, in_, func, reads, writes, scale=None, bias=None, accum_out=None):
        kw = {}
        if scale is not None:
            kw["scale"] = scale
        if bias is not None:
            kw["bias"] = bias
        if accum_out is not None:
            kw["accum_out"] = accum_out
        self.P.op("act", lambda e: e.activation(out=out, in_=in_, func=func, **kw), reads, writes)

    def TT(self, eng, out, in0, in1, op, reads, writes):
        self.P.op(eng, lambda e: e.tensor_tensor(out=out, in0=in0, in1=in1, op=op), reads, writes)

    def TS(self, eng, out, in0, s1, op0, reads, writes, s2=None, op1=None):
        if op1 is None:
            self.P.op(eng, lambda e: e.tensor_scalar(out=out, in0=in0, scalar1=s1, scalar2=None, op0=op0), reads, writes)
        else:
            self.P.op(eng, lambda e: e.tensor_scalar(out=out, in0=in0, scalar1=s1, scalar2=s2, op0=op0, op1=op1), reads, writes)

    def STT(self, out, in0, scalar, in1, op0, op1, reads, writes):
        self.P.op("dve", lambda e: e.scalar_tensor_tensor(out=out, in0=in0, scalar=scalar, in1=in1, op0=op0, op1=op1), reads, writes)

    def CP(self, eng, out, in_, reads, writes):
        if eng == "act":
            self.P.op(eng, lambda e: e.activation(out=out, in_=in_, func=AF.Copy), reads, writes)
        else:
            self.P.op(eng, lambda e: e.tensor_copy(out=out, in_=in_), reads, writes)

    def mcol(self, i, jl, m, k):
        c0 = m * 8 + k
        return self.modT[:, i, jl, c0:c0 + 1]

    def tile_rows(self, ap, t):
        return ap[t * 128:(t + 1) * 128, :]

    def xsrc(self, layer, j, t):
        if layer == 0 and j == 0:
            if t < 32:
                return self.x_in[t * 128:(t + 1) * 128, :], self.b_xsrc[t]
            return self.ctx_in[(t - 32) * 128:(t - 31) * 128, :], self.b_xsrc[t]
        return self.x_d[t * 128:(t + 1) * 128, :], self.b_xd[t]

    def prologue(self):
        P, A = self.P, self.arena
        A.reset()
        sT = self.sT
        b_sT = self.b_sT
        self.DMA("sp", sT[:, :, 0], self.c_in.rearrange("o (k p) -> p (o k)", p=128), [self.b_const], [b_sT], allow_slow_non_contiguous=True)
        self.DMA("sp", sT[:, :, 1], self.cctx_in.rearrange("o (k p) -> p (o k)", p=128), [self.b_const], [b_sT], allow_slow_non_contiguous=True)
        sT2 = sT.rearrange("p k j -> p (k j)")
        self.ACT(sT2, sT2, AF.Silu, [b_sT], [b_sT])
        for st_ in self.mods_job(A, 0):
            st_()
        P.barrier()

    def mods_job(self, A, i):
        NB = 3
        sT, b_sT = self.sT, self.b_sT
        wm = [A.alloc([128, 8, 512], F32) for _ in range(NB)]
        b_wm = [Buf("wm%d" % k) for k in range(NB)]
        bm = [A.alloc([2, 512], F32) for _ in range(NB)]
        b_bm = [Buf() for _ in range(NB)]
        mrow = [A.alloc([2, 512], F32) for _ in range(NB)]
        b_mrow = [Buf() for _ in range(NB)]

        def load_wm(n):
            s = n % NB
            self.DMA("sp", wm[s], self.W["w_mod"][i, :, n * 512:(n + 1) * 512].rearrange("(k p) n -> p k n", p=128), [self.b_w], [b_wm[s]])
            self.DMA("sp", bm[s], self.W["b_mod"][i:i + 1, n * 512:(n + 1) * 512].broadcast_to([2, 512]), [self.b_w], [b_bm[s]])

        def make_step(n):
            def step():
                s = n % NB
                if n == 0:
                    load_wm(0)
                    load_wm(1)
                if n + 2 < 18:
                    load_wm(n + 2)
                ps, pb = self.ps()
                for k in range(8):
                    self.MM(ps[0:2, :], sT[:, k, :], wm[s][:, k, :], k == 0, k == 7, [b_sT, b_wm[s]], pb)
                self.TT("dve", mrow[s], ps[0:2, :], bm[s], ALU.add, [pb, b_bm[s]], [b_mrow[s]])
                self.DMA("sp", self.mods_d[i, :, n * 512:(n + 1) * 512], mrow[s], [b_mrow[s]], [self.b_mods[i]])
                pt, ptb = self.ps()
                for j in range(4):
                    self.TR(pt[:, j * 2:(j + 1) * 2], mrow[s][0:2, j * 128:(j + 1) * 128], self.ident_f[0:2, 0:2], [b_mrow[s], self.b_pc], ptb, j == 0)
                m = n // 2
                c0 = m * 8 + (n % 2) * 4
                self.CP("dve", self.modT[:, i, :, c0:c0 + 4], pt[:, 0:8].rearrange("p (j l) -> p l j", l=2), [ptb], [self.b_modT[i]])
                if n % 2 == 1 and m in (1, 4, 7):
                    v = self.modT[:, i, :, m * 8:(m + 1) * 8]
                    self.TS("dve", v, v, 1.0, ALU.add, [self.b_modT[i]], [self.b_modT[i]])
            return step
        return [make_step(n) for n in range(18)]

    def load_xg(self, layer, j, g, xg, b_xg):
        ntile = 4 if g < 8 else 2
        for tt in range(ntile):
            src, sb_ = self.xsrc(layer, j, g * 4 + tt)
            self.DMA("sp", xg[:, tt, :], src, [sb_], [b_xg[tt]])

    def make_hT(self, layer, j, g, ntile, xg, b_xg, hT, b_hT, m_shift, m_scale, jl):
        N = ntile * 128
        for kq in range(2):
            banks = [self.ps() for _ in range(4)]
            for kk in range(4):
                k = kq * 4 + kk
                ps, pb = banks[kk]
                for tt in range(ntile):
                    self.TR(ps[:, tt * 128:(tt + 1) * 128], xg[:, tt, k * 128:(k + 1) * 128], self.ident_f, [b_xg[tt], self.b_pc], pb, tt == 0)
            for kk in range(4):
                k = kq * 4 + kk
                ps, pb = banks[kk]
                sc = self.mcol(layer, jl, m_scale, k)
                sh = self.mcol(layer, jl, m_shift, k)
                if kk % 2 == 0:
                    self.ACT(hT[:, k, 0:N], ps[:, 0:N], AF.Identity, [pb, self.b_modT[layer]], [b_hT], scale=sc, bias=sh)
                else:
                    self.TS("dve", hT[:, k, 0:N], ps[:, 0:N], sc, ALU.mult, [pb, self.b_modT[layer]], [b_hT], s2=sh, op1=ALU.add)

    def load_w_cast(self, dst, src, wbuf, **kw):
        self.DMA("pool", dst, src, [self.b_w], [wbuf], **kw)

    def ffn_a(self, layer, j):
        P, A = self.P, self.arena
        A.reset()
        mb = 0 if j == 0 else 6
        wu = A.alloc([128, 8, 2 * DFF], BF16)
        b_wu = [Buf("wu%d" % k) for k in range(4)]
        wsrc = self.W["w_ffn_up"][layer, j].rearrange("(k p) n -> p k n", p=128)
        bounds = [0, 6, 12, 17, 22]
        self.wu_part = []
        for q in range(4):
            c0, c1 = bounds[q] * 128, min(bounds[q + 1] * 128, DFF)
            for base in (0, DFF):
                self.load_w_cast(wu[:, :, base + c0:base + c1], wsrc[:, :, base + c0:base + c1], b_wu[q], max_dma_last_dim=8192)
        P.gate_swdge()

        def wbuf_of(c):
            for q in range(4):
                if bounds[q] <= c < bounds[q + 1]:
                    return b_wu[q]
        xg = [A.alloc([128, 4, D], F32) for _ in range(2)]
        b_xg = [[Buf() for _ in range(4)] for _ in range(2)]
        hT = [A.alloc([128, 8, 512], BF16) for _ in range(2)]
        b_hT = [Buf(), Buf()]
        aT = [A.alloc([128, NFC, 512], BF16) for _ in range(2)]
        b_aT = [Buf(), Buf()]
        sg = [A.alloc([128, 512], F32) for _ in range(2)]
        b_sg = [Buf(), Buf()]
        for g in range(9):
            ntile = 4 if g < 8 else 2
            N = ntile * 128
            jl = 0 if g < 8 else 1
            s = g % 2
            if g == 0:
                self.load_xg(layer, j, 0, xg[0], b_xg[0])
            if g + 1 < 9:
                self.load_xg(layer, j, g + 1, xg[1 - s], b_xg[1 - s])
            self.make_hT(layer, j, g, ntile, xg[s], b_xg[s], hT[s], b_hT[s], mb + 0, mb + 1, jl)
            for c in range(NFC):
                wc = 128 if c < NFC - 1 else 64
                pg, pgb = self.ps()
                pu, pub = self.ps()
                wb_ = wbuf_of(c)
                for k in range(8):
                    self.MM(pg[0:wc, 0:N], wu[:, k, c * 128:c * 128 + wc], hT[s][:, k, 0:N], k == 0, k == 7, [wb_, b_hT[s]], pgb)
                for k in range(8):
                    self.MM(pu[0:wc, 0:N], wu[:, k, DFF + c * 128:DFF + c * 128 + wc], hT[s][:, k, 0:N], k == 0, k == 7, [wb_, b_hT[s]], pub)
                ss = c % 2
                self.ACT(sg[ss][0:wc, 0:N], pg[0:wc, 0:N], AF.Silu, [pgb], [b_sg[ss]])
                self.TT("dve", aT[s][0:wc, c, 0:N], sg[ss][0:wc, 0:N], pu[0:wc, 0:N], ALU.mult, [b_sg[ss], pub], [b_aT[s]])
            t0 = g * 512
            self.DMA("sp", self.aT_d[0:NFC - 1, :, t0:t0 + N].rearrange("c p t -> p c t"), aT[s][:, 0:NFC - 1, 0:N], [b_aT[s]], [self.b_aT[g]])
            self.DMA("sp", self.aT_d[NFC - 1, 0:64, t0:t0 + N], aT[s][0:64, NFC - 1, 0:N], [b_aT[s]], [self.b_aT[g]])
        P.barrier()

    def epilogue(self, halves, xt, b_xt, G, b_G, lng, lnb, b_ln, t1, b_t1, y, b_y, st, b_st, otile, b_ot):
        for h in range(2):
            ps, pb = halves[h]
            self.TT("dve", t1[:, h * 512:(h + 1) * 512], ps, G[:, h * 512:(h + 1) * 512], ALU.mult, [pb, b_G], [b_t1])
        self.STT(y, xt, ALPHA, t1, ALU.mult, ALU.add, [b_xt, b_t1], [b_y])
        stats, mv, sd, rstd, nb = st
        for h in range(2):
            self.P.op("dve", lambda e, h=h: e.bn_stats(out=stats[:, h * 6:(h + 1) * 6], in_=y[:, h * 512:(h + 1) * 512]), [b_y], [b_st])
        self.P.op("dve", lambda e: e.bn_aggr(out=mv, in_=stats), [b_st], [b_st])
        self.ACT(sd, mv[:, 1:2], AF.Sqrt, [b_st, self.b_pc], [b_st], bias=self.eps_t[:, 0:1])
        self.P.op("dve", lambda e: e.reciprocal(out=rstd, in_=sd), [b_st], [b_st])
        self.TS("dve", nb, mv[:, 0:1], -1.0, ALU.mult, [b_st], [b_st], s2=rstd[:, 0:1], op1=ALU.mult)
        self.ACT(t1, y, AF.Identity, [b_y, b_st], [b_t1], scale=rstd[:, 0:1], bias=nb[:, 0:1])
        self.TT("dve", otile, t1, lng, ALU.mult, [b_t1, b_ln], [b_ot])
        self.TT("dve", otile, otile, lnb, ALU.add, [b_ot, b_ln], [b_ot])
        assert otile is y and b_ot is b_y

    def load_epi_consts(self, A, layer, gate_m, lnidx, half):
        G = A.alloc([128, D], F32)
        Gc = A.alloc([128, D], F32)
        lng = A.alloc([128, D], F32)
        lnb = A.alloc([128, D], F32)
        b_G, b_Gc, b_ln = Buf(), Buf(), Buf()
        self.DMA("sp", G, self.mods_d[layer, 0:1, gate_m * D:(gate_m + 1) * D].broadcast_to([128, D]), [self.b_mods[layer]], [b_G])
        self.DMA("sp", Gc, self.mods_d[layer, 1:2, gate_m * D:(gate_m + 1) * D].broadcast_to([128, D]), [self.b_mods[layer]], [b_Gc])
        self.DMA("sp", lng, self.W["ln_g"][layer, lnidx:lnidx + 1, :].broadcast_to([128, D]), [self.b_w], [b_ln])
        self.DMA("sp", lnb, self.W["ln_b"][layer, lnidx:lnidx + 1, :].broadcast_to([128, D]), [self.b_w], [b_ln])
        if half:
            self.P.op("act", lambda e: e.mul(out=G, in_=G, mul=0.5), [b_G], [b_G])
            self.P.op("act", lambda e: e.mul(out=Gc, in_=Gc, mul=0.5), [b_Gc], [b_Gc])
        return G, b_G, Gc, b_Gc, lng, lnb, b_ln

    def alloc_epi_work(self, A, n):
        work = []
        for _ in range(n):
            t1 = A.alloc([128, D], F32)
            y = A.alloc([128, D], F32)
            st = (A.alloc([128, 12], F32), A.alloc([128, 2], F32), A.alloc([128, 1], F32), A.alloc([128, 1], F32), A.alloc([128, 1], F32))
            b_y_ = Buf()
            work.append((t1, Buf(), y, b_y_, st, Buf(), y, b_y_))
        return work

    def xdst(self, layer, j, t, final):
        if final and t < 32:
            return self.out[t * 128:(t + 1) * 128, :], self.b_xd[t]
        return self.x_d[t * 128:(t + 1) * 128, :], self.b_xd[t]

    def ffn_b(self, layer, j, ngroups=9, final=False):
        P, A = self.P, self.arena
        A.reset()
        mb = 0 if j == 0 else 6
        wd = A.alloc([128, NFC, D], BF16)
        b_wd = Buf("wd")
        wsrc = self.W["w_ffn_down"][layer, j]
        self.load_w_cast(wd[:, 0:NFC - 1, :], wsrc[0:(NFC - 1) * 128, :].rearrange("(k p) n -> p k n", p=128), b_wd)
        self.load_w_cast(wd[0:64, NFC - 1, :], wsrc[(NFC - 1) * 128:DFF, :], b_wd)
        P.gate_swdge()
        G, b_G, Gc, b_Gc, lng, lnb, b_ln = self.load_epi_consts(A, layer, mb + 2, 0 if j == 0 else 2, True)
        aTs = [A.alloc([128, NFC, 512], BF16) for _ in range(2)]
        b_aTs = [Buf(), Buf()]
        NX = 4
        xt = [A.alloc([128, D], F32) for _ in range(NX)]
        b_xt = [Buf() for _ in range(NX)]
        NWK = 3
        work = self.alloc_epi_work(A, NWK)
        ntiles_total = sum(4 if g < 8 else 2 for g in range(ngroups))

        def load_aT(g):
            ntile = 4 if g < 8 else 2
            N = ntile * 128
            s = g % 2
            t0 = g * 512
            self.DMA("sp", aTs[s][:, 0:NFC - 1, 0:N], self.aT_d[0:NFC - 1, :, t0:t0 + N].rearrange("c p t -> p c t"), [self.b_aT[g]], [b_aTs[s]])
            self.DMA("sp", aTs[s][0:64, NFC - 1, 0:N], self.aT_d[NFC - 1, 0:64, t0:t0 + N], [self.b_aT[g]], [b_aTs[s]])

        def load_x(t):
            src, sb_ = self.xsrc(layer, j, t)
            self.DMA("sp", xt[t % NX], src, [sb_], [b_xt[t % NX]])

        load_aT(0)
        load_x(0)
        load_x(1)
        it = 0
        for g in range(ngroups):
            ntile = 4 if g < 8 else 2
            N = ntile * 128
            s = g % 2
            t0 = g * 512
            if g + 1 < ngroups:
                load_aT(g + 1)
            for tt in range(ntile):
                t = g * 4 + tt
                xs = t % NX
                ws = it % NWK
                it += 1
                if t + 2 < ntiles_total:
                    load_x(t + 2)
                halves = []
                for h in range(2):
                    ps, pb = self.ps()
                    for c in range(NFC):
                        kk = 128 if c < NFC - 1 else 64
                        self.MM(ps, aTs[s][0:kk, c, tt * 128:(tt + 1) * 128], wd[0:kk, c, h * 512:(h + 1) * 512], c == 0, c == NFC - 1, [b_aTs[s], b_wd], pb)
                    halves.append((ps, pb))
                t1, b_t1, y, b_y, st, b_st, ot, b_ot = work[ws]
                self.epilogue(halves, xt[xs], b_xt[xs], G if g < 8 else Gc, b_G if g < 8 else b_Gc, lng, lnb, b_ln, t1, b_t1, y, b_y, st, b_st, ot, b_ot)
                dst, db = self.xdst(layer, j, t, final)
                self.DMA("sp", dst, ot, [b_ot], [db])
        P.barrier()

    def mix_p(self, layer, need_ctx):
        P, A = self.P, self.arena
        A.reset()
        W = self.W
        win = A.alloc([128, 8, IN_W], BF16)
        b_win = Buf("win")
        wsrc = W["w_in"][layer].rearrange("(k p) n -> p k n", p=128)
        self.load_w_cast(win[:, :, 0:1152], wsrc[:, :, 0:1152], b_win)
        self.load_w_cast(win[:, :, 1152:IN_W], wsrc[:, :, 1152:IN_W], b_win)
        P.gate_swdge()
        wperm = A.alloc([128, 8, 640], BF16)
        b_wperm = Buf("wperm")
        wv = win[:, :, 0:640].rearrange("p k (h s e) -> p k h s e", s=2, e=16)
        pv = wperm.rearrange("p k (h s e) -> p k h s e", s=2, e=16)
        for k in range(8):
            self.CP("act", pv[:, k, :, 0, :], wv[:, k, :, 1, :], [b_win], [b_wperm])
            self.CP("act", pv[:, k, :, 1, :], wv[:, k, :, 0, :], [b_win], [b_wperm])
        wsn = A.alloc([128, 4, 128], F32)
        b_wsn = Buf()
        self.DMA("sp", wsn, W["sgu_w"][layer].rearrange("g p q -> p g q"), [self.b_w], [b_wsn])
        wsT = A.alloc([128, 4, 128], BF16)
        b_wsT = Buf()
        ps, pb = self.ps()
        for gq in range(4):
            self.TR(ps[:, gq * 128:(gq + 1) * 128], wsn[:, gq, :], self.ident_f, [b_wsn, self.b_pc], pb, gq == 0)
        self.CP("dve", wsT.rearrange("p g q -> p (g q)"), ps, [pb], [b_wsT])
        sbr = A.alloc([1, 4, 128], F32)
        b_sbr = Buf()
        self.DMA("sp", sbr, W["sgu_b"][layer:layer + 1], [self.b_w], [b_sbr])
        slg = A.alloc([128, 512], F32)
        slb = A.alloc([128, 512], F32)
        b_sl = Buf()
        self.DMA("sp", slg, W["sgu_ln_g"][layer:layer + 1, :].broadcast_to([128, 512]), [self.b_w], [b_sl])
        self.DMA("sp", slb, W["sgu_ln_b"][layer:layer + 1, :].broadcast_to([128, 512]), [self.b_w], [b_sl])

        xg = [A.alloc([128, 4, D], F32) for _ in range(2)]
        b_xg = [[Buf() for _ in range(4)] for _ in range(2)]
        hT = [A.alloc([128, 8, 512], BF16) for _ in range(2)]
        b_hT = [Buf(), Buf()]
        rc = [A.alloc([128, 512], F32) for _ in range(2)]
        rs = [A.alloc([128, 512], F32) for _ in range(2)]
        b_rt = [Buf(), Buf()]
        qk = [A.alloc([128, 5, 512], BF16) for _ in range(2)]
        b_qk = [Buf(), Buf()]
        r1 = [A.alloc([128, 512], F32) for _ in range(2)]
        r2 = [A.alloc([128, 512], F32) for _ in range(2)]
        b_r = [Buf(), Buf()]
        uT = [A.alloc([128, 4, 512], F32) for _ in range(2)]
        b_uT = [Buf(), Buf()]
        gtmp = [A.alloc([128, 512], F32) for _ in range(2)]
        b_gt = [Buf(), Buf()]
        sgT = [A.alloc([128, 4, 512], BF16) for _ in range(2)]
        b_sgT = [Buf(), Buf()]
        vt = [A.alloc([128, 4, 128], BF16) for _ in range(2)]
        b_vt = [Buf(), Buf()]
        ft = [A.alloc([128, 4, 512], BF16) for _ in range(2)]
        b_ft = [Buf(), Buf()]
        NZ = 4
        zt = [A.alloc([128, 512], F32) for _ in range(NZ)]
        zg = [A.alloc([128, 512], F32) for _ in range(NZ)]
        zn = [A.alloc([128, 512], BF16) for _ in range(NZ)]
        zst = [(A.alloc([128, 6], F32), A.alloc([128, 2], F32), A.alloc([128, 1], F32), A.alloc([128, 1], F32), A.alloc([128, 1], F32)) for _ in range(NZ)]
        b_z = [Buf() for _ in range(NZ)]
        b_zn = [Buf() for _ in range(NZ)]

        def gelu(eng_mul, out, ps_in, tmp, reads, b_tmp, writes):
            self.ACT(out, ps_in, AF.Gelu_apprx_tanh, reads, writes)

        zi = 0
        for g in range(9):
            ctxg = g == 8
            ntile = 2 if ctxg else 4
            N = ntile * 128
            jl = 1 if ctxg else 0
            s = g % 2
            t0 = g * 512
            if g == 0:
                self.load_xg(layer, 1, 0, xg[0], b_xg[0])
                self.DMA("sp", rc[0], self.C["ropec"][:, 0:512], [self.b_const], [b_rt[0]])
                self.DMA("sp", rs[0], self.C["ropes"][:, 0:512], [self.b_const], [b_rt[0]])
            if g + 1 < 9:
                self.load_xg(layer, 1, g + 1, xg[1 - s], b_xg[1 - s])
                if g + 1 < 8:
                    self.DMA("sp", rc[1 - s], self.C["ropec"][:, t0 + 512:t0 + 1024], [self.b_const], [b_rt[1 - s]])
                    self.DMA("sp", rs[1 - s], self.C["ropes"][:, t0 + 512:t0 + 1024], [self.b_const], [b_rt[1 - s]])
            self.make_hT(layer, 1, g, ntile, xg[s], b_xg[s], hT[s], b_hT[s], 3, 4, jl)
            if need_ctx or not ctxg:
                self.DMA("sp", self.hT_d[:, :, t0:t0 + N].rearrange("k p t -> p k t"), hT[s][:, :, 0:N], [b_hT[s]], [self.b_hT[g]])
            for hp in range(5):
                if ctxg and hp < 4 and not need_ctx:
                    continue
                c0 = hp * 128
                pa, pab = self.ps()
                for k in range(8):
                    self.MM(pa[:, 0:N], win[:, k, c0:c0 + 128], hT[s][:, k, 0:N], k == 0, k == 7, [b_win, b_hT[s]], pab)
                if ctxg:
                    self.CP("dve", qk[s][:, hp, 0:N], pa[:, 0:N], [pab], [b_qk[s]])
                    continue
                pp, ppb = self.ps()
                for k in range(8):
                    self.MM(pp[:, 0:N], wperm[:, k, c0:c0 + 128], hT[s][:, k, 0:N], k == 0, k == 7, [b_wperm, b_hT[s]], ppb)
                rr = hp % 2
                self.TT("dve", r1[rr], pa, rc[s], ALU.mult, [pab, b_rt[s]], [b_r[rr]])
                self.TT("dve", r2[rr], pp, rs[s], ALU.mult, [ppb, b_rt[s]], [b_r[rr]])
                self.TT("dve", qk[s][:, hp, :], r1[rr], r2[rr], ALU.add, [b_r[rr]], [b_qk[s]])
            for hp in range(5):
                if ctxg and hp < 4 and not need_ctx:
                    continue
                for hh in range(2):
                    if hp < 4:
                        dst = self.qT_d[2 * hp + hh, :, t0:t0 + N]
                        db = self.b_q[g]
                    else:
                        dst = self.kT_d[hh, :, t0:t0 + N]
                        db = self.b_k[g]
                    self.DMA("sp", dst, qk[s][hh * 64:(hh + 1) * 64, hp, 0:N], [b_qk[s]], [db])
            for tt in range(ntile):
                pvv, pvb = self.ps()
                for k in range(8):
                    self.MM(pvv[:, 0:128], hT[s][:, k, tt * 128:(tt + 1) * 128], win[:, k, 640:768], k == 0, k == 7, [b_win, b_hT[s]], pvb)
                self.CP("act", vt[s][:, tt, :], pvv[:, 0:128], [pvb], [b_vt[s]])
            self.DMA("sp", self.v_d[t0:t0 + N, :].rearrange("(t p) c -> p t c", p=128), vt[s][:, 0:ntile, :], [b_vt[s]], [self.b_v[g]])
            if not ctxg:
                for tt in range(ntile):
                    pf, pfb = self.ps()
                    for k in range(8):
                        self.MM(pf, hT[s][:, k, tt * 128:(tt + 1) * 128], win[:, k, 1792:2304], k == 0, k == 7, [b_win, b_hT[s]], pfb)
                    self.CP("act", ft[s][:, tt, :], pf, [pfb], [b_ft[s]])
                self.DMA("sp", self.f_d[t0:t0 + N, :].rearrange("(t p) c -> p t c", p=128), ft[s], [b_ft[s]], [self.b_f[g]])
            elif need_ctx:
                for cc in range(4):
                    pf, pfb = self.ps()
                    for k in range(8):
                        self.MM(pf[:, 0:N], win[:, k, 1792 + cc * 128:1792 + (cc + 1) * 128], hT[s][:, k, 0:N], k == 0, k == 7, [b_win, b_hT[s]], pfb)
                    self.CP("act", ft[s][:, cc, 0:N], pf[:, 0:N], [pfb], [b_ft[s]])
                self.DMA("sp", self.fcT_d.rearrange("c p t -> p c t"), ft[s][:, :, 0:N], [b_ft[s]], [self.b_fc])
            if ctxg and not need_ctx:
                continue
            for tt in range(ntile):
                zs = tt
                pz, pzb = self.ps()
                for k in range(8):
                    self.MM(pz, hT[s][:, k, tt * 128:(tt + 1) * 128], win[:, k, 1280:1792], k == 0, k == 7, [b_win, b_hT[s]], pzb)
                gelu("dve", zg[zs], pz, zt[zs], [pzb], b_z[zs], [b_z[zs]])
                stats, mv, sd, rstd, nb = zst[zs]
                self.P.op("dve", lambda e, stats=stats, zz=zg[zs]: e.bn_stats(out=stats, in_=zz), [b_z[zs]], [b_z[zs]])
                self.P.op("dve", lambda e, stats=stats, mv=mv: e.bn_aggr(out=mv, in_=stats), [b_z[zs]], [b_z[zs]])
                self.ACT(sd, mv[:, 1:2], AF.Sqrt, [b_z[zs], self.b_pc], [b_z[zs]], bias=self.eps_t[:, 0:1])
                self.P.op("dve", lambda e, rstd=rstd, sd=sd: e.reciprocal(out=rstd, in_=sd), [b_z[zs]], [b_z[zs]])
                self.TS("dve", nb, mv[:, 0:1], -1.0, ALU.mult, [b_z[zs]], [b_z[zs]], s2=rstd[:, 0:1], op1=ALU.mult)
                self.ACT(zt[zs], zg[zs], AF.Identity, [b_z[zs]], [b_z[zs]], scale=rstd[:, 0:1], bias=nb[:, 0:1])
                self.TT("dve", zt[zs], zt[zs], slg, ALU.mult, [b_z[zs], b_sl], [b_z[zs]])
                self.TT("dve", zn[zs], zt[zs], slb, ALU.add, [b_z[zs], b_sl], [b_zn[zs]])
            for cc in range(4):
                pu, pub = self.ps()
                for k in range(8):
                    self.MM(pu[:, 0:N], win[:, k, 768 + cc * 128:768 + (cc + 1) * 128], hT[s][:, k, 0:N], k == 0, k == 7, [b_win, b_hT[s]], pub)
                gs = cc % 2
                gelu("dve", uT[s][:, cc, 0:N], pu[:, 0:N], gtmp[gs][:, 0:N], [pub], b_gt[gs], [b_uT[s]])
            for tt in range(ntile):
                zs = tt
                pm, pmb = self.ps()
                for gq in range(4):
                    self.MM(pm[:, gq * 128:(gq + 1) * 128], zn[zs][:, gq * 128:(gq + 1) * 128], wsT[:, gq, :], True, False, [b_zn[zs], b_wsT], pmb)
                    self.MM(pm[:, gq * 128:(gq + 1) * 128], self.ones_f[0:1, :], sbr[0:1, gq, :], False, True, [self.b_pc, b_sbr], pmb)
                self.TT("dve", sgT[s][:, :, tt * 128:(tt + 1) * 128], pm.rearrange("p (g q) -> p g q", g=4), uT[s][:, :, tt * 128:(tt + 1) * 128], ALU.mult,
                        [pmb, b_uT[s]], [b_sgT[s]])
            self.DMA("sp", self.brT_d[1][:, :, t0:t0 + N].rearrange("c p t -> p c t"), sgT[s][:, :, 0:N], [b_sgT[s]], [self.b_br[1][g]])
        P.barrier()

    def att(self, layer, need_ctx):
        P, A = self.P, self.arena
        A.reset()
        kTa = A.alloc([64, 2, T], BF16)
        b_kTa = Buf()
        self.DMA("sp", kTa, self.kT_d.rearrange("h d t -> d h t"), self.b_k, [b_kTa])
        vall = A.alloc([128, NT, 128], BF16)
        b_vall = Buf()
        self.DMA("sp", vall, self.v_d.rearrange("(t p) c -> p t c", p=128), self.b_v, [b_vall])
        maskp = A.alloc([128, 512], BF16)
        maskn = A.alloc([128, 512], BF16)
        b_mask = Buf()
        self.DMA("sp", maskp, self.C["maskp"], [self.b_const], [b_mask])
        self.DMA("sp", maskn, self.C["maskn"], [self.b_const], [b_mask])
        sk = A.alloc([1, 8], F32)
        b_sk = Buf()
        self.DMA("sp", sk, self.W["attn_sink"][layer:layer + 1, :], [self.b_w], [b_sk])
        self.ACT(sk, sk, AF.Exp, [b_sk], [b_sk])
        skr = A.alloc([1, 8, 128], F32)
        b_skr = Buf()
        self.CP("dve", skr, sk.rearrange("o (h i) -> o h i", i=1).broadcast_to([1, 8, 128]), [b_sk], [b_skr])
        qa = [A.alloc([64, 8, 512], BF16) for _ in range(2)]
        b_qa = [Buf(), Buf()]
        oall = [A.alloc([64, 8, 512], BF16) for _ in range(2)]
        b_oall = [Buf(), Buf()]
        NPT = 12
        pT = [A.alloc([128, 512], BF16) for _ in range(NPT)]
        b_pT = [Buf() for _ in range(NPT)]
        rden = [A.alloc([64, 512], F32) for _ in range(2)]
        b_rden = [Buf(), Buf()]
        st = {"pi": 0, "ri": 0}
        ngroups = 9 if need_ctx else 8

        def load_q(g):
            N = 256 if g == 8 else 512
            t0 = g * 512
            self.DMA("sp", qa[g % 2][:, :, 0:N], self.qT_d[:, :, t0:t0 + N].rearrange("h d t -> d h t"), [self.b_q[g]], [b_qa[g % 2]])

        def pv_stage(pts, kh, s, bl):
            po, pob = self.ps()
            pd, pdb = self.ps()
            n = len(pts)
            for ci, (kt, pp) in enumerate(pts):
                self.MM(po[0:64, :], vall[:, kt, kh * 64:(kh + 1) * 64], pT[pp], ci == 0, ci == n - 1, [b_vall, b_pT[pp]], pob)
                self.MM(pd[0:64, :], self.ones_b, pT[pp], ci == 0, False, [self.b_pc, b_pT[pp]], pdb)
            self.MM(pd[0:64, :], self.ones_f[0:1, 0:64], skr[0:1, kh * 4:(kh + 1) * 4, :], False, True, [self.b_pc, b_skr], pdb)
            rr = st["ri"] % 2
            st["ri"] += 1
            self.P.op("dve", lambda e, o=rden[rr], i_=pd[0:64, :]: e.reciprocal(out=o, in_=i_), [pdb], [b_rden[rr]])
            self.TT("dve", oall[s][:, kh * 4:(kh + 1) * 4, bl * 128:(bl + 1) * 128], po[0:64, :].rearrange("p (g q) -> p g q", g=4),
                    rden[rr].rearrange("p (g q) -> p g q", g=4), ALU.mult, [pob, b_rden[rr]], [b_oall[s]])

        side = self.mods_job(A, layer + 1) if layer + 1 < self.wl else []
        side_i = [0, 0]

        def side_tick():
            side_i[1] += 1
            if side_i[1] % 3 == 0 and side_i[0] < len(side):
                side[side_i[0]]()
                side_i[0] += 1

        load_q(0)
        for g in range(ngroups):
            ctxg = g == 8
            nblk = 2 if ctxg else 4
            N = nblk * 128
            s = g % 2
            t0 = g * 512
            if g + 1 < ngroups:
                load_q(g + 1)
            pending = None
            for bl in range(nblk):
                n = g * 4 + bl
                if ctxg:
                    chunks = [(32, None), (33, None)]
                else:
                    chunks = []
                    if n > 0:
                        chunks.append((n - 1, maskp))
                    chunks.append((n, None))
                    if n < 31:
                        chunks.append((n + 1, maskn))
                    chunks += [(32, None), (33, None)]
                for kh in range(2):
                    rhs_q = qa[s][:, kh * 4:(kh + 1) * 4, bl * 128:(bl + 1) * 128]
                    pts = []
                    for ci, (kt, mask) in enumerate(chunks):
                        pst, pstb = self.ps()
                        self.MM(pst, kTa[:, kh, kt * 128:(kt + 1) * 128], rhs_q, True, mask is None, [b_kTa, b_qa[s]], pstb)
                        if mask is not None:
                            self.MM(pst, self.ident_b, mask, False, True, [self.b_pc, b_mask], pstb)
                        pp = st["pi"] % NPT
                        st["pi"] += 1
                        self.ACT(pT[pp], pst, AF.Exp, [pstb], [b_pT[pp]], scale=0.125)
                        pts.append((kt, pp))
                    if pending is not None:
                        pv_stage(*pending)
                    pending = (pts, kh, s, bl)
                    side_tick()
            pv_stage(*pending)
            self.DMA("sp", self.brT_d[0][:, :, t0:t0 + N].rearrange("h d t -> d h t"), oall[s][:, :, 0:N], [b_oall[s]], [self.b_br[0][g]])
        while side_i[0] < len(side):
            side[side_i[0]]()
            side_i[0] += 1
        P.barrier()

    def fft(self, layer, need_ctx):
        P, A = self.P, self.arena
        A.reset()
        Cn = self.C
        cst = {}
        b_c = Buf()
        for nm in ("c128", "s128n", "s128", "w2"):
            cst[nm] = A.alloc([128, 128], BF16)
            self.DMA("sp", cst[nm], Cn[nm], [self.b_const], [b_c])
        tw = A.alloc([128, 64], F32)
        self.DMA("sp", tw, Cn["tw"], [self.b_const], [b_c])
        fall = A.alloc([128, 32, 512], BF16)
        b_fall = [Buf() for _ in range(4)]
        fsrc = self.f_d.rearrange("(a b) m -> a b m", b=32)
        for q in range(4):
            self.DMA("sp", fall[:, q * 8:(q + 1) * 8, :], fsrc[:, q * 8:(q + 1) * 8, :], self.b_f, [b_fall[q]])
        Bst = [A.alloc([128, 2, 4, 512], BF16) for _ in range(2)]
        b_Bst = [Buf(), Buf()]
        tt1 = [A.alloc([128, 512], F32) for _ in range(2)]
        tt2 = [A.alloc([128, 512], F32) for _ in range(2)]
        b_tt = [Buf(), Buf()]
        for n2 in range(32):
            q4 = n2 // 4
            s = q4 % 2
            par, parb = self.ps()
            pai, paib = self.ps()
            self.MM(par, cst["c128"], fall[:, n2, :], True, True, [b_c, b_fall[n2 // 8]], parb)
            self.MM(pai, cst["s128n"], fall[:, n2, :], True, True, [b_c, b_fall[n2 // 8]], paib)
            tc = tw[:, n2:n2 + 1]
            tsn = tw[:, 32 + n2:33 + n2]
            x = n2 % 2
            self.ACT(tt1[x], pai, AF.Identity, [paib, b_c], [b_tt[x]], scale=tsn)
            self.ACT(tt2[x], par, AF.Identity, [parb, b_c], [b_tt[x]], scale=tsn)
            self.STT(Bst[s][:, 0, n2 % 4, :], par, tc, tt1[x], ALU.mult, ALU.add, [parb, b_c, b_tt[x]], [b_Bst[s]])
            self.STT(Bst[s][:, 1, n2 % 4, :], pai, tc, tt2[x], ALU.mult, ALU.subtract, [paib, b_c, b_tt[x]], [b_Bst[s]])
            if n2 % 4 == 3:
                self.DMA("sp", self.B_d[:, :, q4 * 4:(q4 + 1) * 4, :], Bst[s], [b_Bst[s]], [self.b_Bd[q4]])
        P.barrier()
        A.off = 0
        for nm in ("c128", "s128n", "s128", "w2"):
            A.alloc([128, 128], BF16)
        A.alloc([128, 64], F32)
        Bs = A.alloc([128, 64, 512], BF16)
        b_Bs = [Buf() for _ in range(4)]
        bsrc = self.B_d.rearrange("(kp lo) r n c -> (lo r n) kp c", lo=2)
        for q in range(4):
            self.DMA("sp", Bs[:, q * 16:(q + 1) * 16, :], bsrc[:, q * 16:(q + 1) * 16, :], self.b_Bd, [b_Bs[q]])
        XrT = A.alloc([128, 4, SEQ], BF16)
        XiT = A.alloc([128, 4, SEQ], BF16)
        b_X = Buf()
        ev = 0
        for kq in range(16):
            for cc in range(4):
                ps, pb = self.ps()
                for kpl in range(4):
                    kp = kq * 4 + kpl
                    self.MM(ps[:, kpl * 128:(kpl + 1) * 128], Bs[:, kp, cc * 128:(cc + 1) * 128], cst["w2"], True, True, [b_Bs[kp // 16], b_c], pb)
                pv_ = ps.rearrange("p (a r k) -> p a r k", r=2, k=32)
                for rix, XT in ((0, XrT), (1, XiT)):
                    dst = XT[:, cc, :].rearrange("p (k2 k1) -> p k1 k2", k1=128)[:, kq * 8:(kq + 1) * 8, :]
                    eng = "act" if ev % 2 == 0 else "dve"
                    ev += 1
                    if eng == "act":
                        self.ACT(dst, pv_[:, :, rix, :], AF.Copy, [pb], [b_X])
                    else:
                        self.CP("dve", dst, pv_[:, :, rix, :], [pb], [b_X])
        fo = [A.alloc([128, 4, 512], BF16) for _ in range(2)]
        b_fo = [Buf(), Buf()]
        for g in range(8):
            s = g % 2
            for cc in range(4):
                ps, pb = self.ps()
                self.MM(ps, cst["c128"], XrT[:, cc, g * 512:(g + 1) * 512], True, False, [b_c, b_X], pb)
                self.MM(ps, cst["s128"], XiT[:, cc, g * 512:(g + 1) * 512], False, True, [b_c, b_X], pb)
                if cc % 2 == 0:
                    self.ACT(fo[s][:, cc, :], ps, AF.Copy, [pb], [b_fo[s]])
                else:
                    self.CP("dve", fo[s][:, cc, :], ps, [pb], [b_fo[s]])
            self.DMA("sp", self.brT_d[2][:, :, g * 512:(g + 1) * 512].rearrange("c p t -> p c t"), fo[s], [b_fo[s]], [self.b_br[2][g]])
        if need_ctx:
            ccscn = A.alloc([128, 256], BF16)
            c256 = A.alloc([128, 2, 256], BF16)
            s256 = A.alloc([128, 2, 256], BF16)
            b_cc = Buf()
            self.DMA("sp", ccscn, Cn["ccscn"], [self.b_const], [b_cc])
            self.DMA("sp", c256, Cn["c256"], [self.b_const], [b_cc])
            self.DMA("sp", s256, Cn["s256"], [self.b_const], [b_cc])
            fc = A.alloc([128, 4, CTX], BF16)
            b_fcs = Buf()
            self.DMA("sp", fc, self.fcT_d.rearrange("c p t -> p c t"), [self.b_fc], [b_fcs])
            Gs = A.alloc([128, 2, 4, 256], BF16)
            b_Gs = Buf()
            for tt in range(2):
                for cc in range(4):
                    ps, pb = self.ps()
                    self.MM(ps[:, 0:256], fc[:, cc, tt * 128:(tt + 1) * 128], ccscn, True, True, [b_fcs, b_cc], pb)
                    self.CP("dve", Gs[:, tt, cc, :], ps[:, 0:256], [pb], [b_Gs])
            foc = A.alloc([128, 4, CTX], BF16)
            b_foc = Buf()
            for cc in range(4):
                ps, pb = self.ps()
                for tt in range(2):
                    self.MM(ps[:, 0:256], Gs[:, tt, cc, 0:128], c256[:, tt, :], tt == 0, False, [b_Gs, b_cc], pb)
                    self.MM(ps[:, 0:256], Gs[:, tt, cc, 128:256], s256[:, tt, :], False, tt == 1, [b_Gs, b_cc], pb)
                self.CP("dve", foc[:, cc, :], ps[:, 0:256], [pb], [b_foc])
            self.DMA("sp", self.brT_d[2][:, :, SEQ:T].rearrange("c p t -> p c t"), foc, [b_foc], [self.b_br[2][8]])
        P.barrier()

    def merge(self, layer, need_ctx):
        P, A = self.P, self.arena
        A.reset()
        W = self.W
        wg = A.alloc([128, 3, 8, D], BF16)
        b_wg = [Buf() for _ in range(3)]
        for r in range(3):
            self.load_w_cast(wg[:, r], W["w_gate"][layer, r].rearrange("(k p) n -> p k n", p=128), b_wg[r])
        wbr = A.alloc([128, 3, 4, D], BF16)
        b_wb = Buf()
        for r in range(3):
            self.load_w_cast(wbr[:, r], W["w_branch"][layer, r].rearrange("(k p) n -> p k n", p=128), b_wb)
        att4 = self.brT_d[0].rearrange("(c two) d t -> c (two d) t", two=2)
        wo = A.alloc([128, 8, D], BF16)
        b_wo = Buf()
        self.load_w_cast(wo, W["w_out"][layer].rearrange("(k p) n -> p k n", p=128), b_wo)
        P.gate_swdge()
        G, b_G, Gc, b_Gc, lng, lnb, b_ln = self.load_epi_consts(A, layer, 5, 1, False)
        hT = [A.alloc([128, 8, 512], BF16) for _ in range(2)]
        b_hT = [Buf(), Buf()]
        at = [A.alloc([128, 4, 512], BF16) for _ in range(2)]
        sg = [A.alloc([128, 4, 512], BF16) for _ in range(2)]
        fo = [A.alloc([128, 4, 512], BF16) for _ in range(2)]
        b_br = [[Buf(), Buf()] for _ in range(3)]
        mT = A.alloc([128, 8, 512], BF16)
        b_mT = Buf()
        sig = [A.alloc([128, 512], F32) for _ in range(3)]
        b_sig = [Buf() for _ in range(3)]
        acc = [A.alloc([128, 512], F32) for _ in range(2)]
        b_acc = [Buf(), Buf()]
        tmp = [A.alloc([128, 512], F32) for _ in range(2)]
        b_tmp = [Buf(), Buf()]
        xt = [A.alloc([128, D], F32) for _ in range(2)]
        b_xt = [Buf() for _ in range(2)]
        work = self.alloc_epi_work(A, 1)
        it = 0
        si = 0
        ngroups = 9 if need_ctx else 8
        ntiles_total = sum(4 if g < 8 else 2 for g in range(ngroups))

        def load_grp(g):
            ntile = 4 if g < 8 else 2
            N = ntile * 128
            s = g % 2
            t0 = g * 512
            self.DMA("sp", hT[s][:, :, 0:N], self.hT_d[:, :, t0:t0 + N].rearrange("k p t -> p k t"), [self.b_hT[g]], [b_hT[s]])
            self.DMA("sp", at[s][:, :, 0:N], att4[:, :, t0:t0 + N].rearrange("c p t -> p c t"), [self.b_br[0][g]], [b_br[0][s]])
            self.DMA("sp", sg[s][:, :, 0:N], self.brT_d[1][:, :, t0:t0 + N].rearrange("c p t -> p c t"), [self.b_br[1][g]], [b_br[1][s]])
            self.DMA("sp", fo[s][:, :, 0:N], self.brT_d[2][:, :, t0:t0 + N].rearrange("c p t -> p c t"), [self.b_br[2][g]], [b_br[2][s]])

        def load_x(t):
            src, sb_ = self.xsrc(layer, 1, t)
            self.DMA("sp", xt[t % 2], src, [sb_], [b_xt[t % 2]])

        load_grp(0)
        load_x(0)
        for g in range(ngroups):
            ntile = 4 if g < 8 else 2
            N = ntile * 128
            s = g % 2
            t0 = g * 512
            if g + 1 < ngroups:
                load_grp(g + 1)
            for fc in range(8):
                a = fc % 2
                for r in range(3):
                    pg, pgb = self.ps()
                    for k in range(8):
                        self.MM(pg[:, 0:N], wg[:, r, k, fc * 128:(fc + 1) * 128], hT[s][:, k, 0:N], k == 0, k == 7, [b_wg[r], b_hT[s]], pgb)
                    pbr, pbb = self.ps()
                    src = (at, sg, fo)[r]
                    for k in range(4):
                        self.MM(pbr[:, 0:N], wbr[:, r, k, fc * 128:(fc + 1) * 128], src[s][:, k, 0:N], k == 0, k == 3, [b_wb, b_br[r][s]], pbb)
                    ss = si % 3
                    si += 1
                    self.ACT(sig[ss][:, 0:N], pg[:, 0:N], AF.Sigmoid, [pgb], [b_sig[ss]])
                    if r == 0:
                        self.TT("dve", acc[a][:, 0:N], sig[ss][:, 0:N], pbr[:, 0:N], ALU.mult, [b_sig[ss], pbb], [b_acc[a]])
                    else:
                        x_ = (r - 1)
                        self.TT("dve", tmp[x_][:, 0:N], sig[ss][:, 0:N], pbr[:, 0:N], ALU.mult, [b_sig[ss], pbb], [b_tmp[x_]])
                        if r == 1:
                            self.TT("dve", acc[a][:, 0:N], acc[a][:, 0:N], tmp[x_][:, 0:N], ALU.add, [b_acc[a], b_tmp[x_]], [b_acc[a]])
                        else:
                            self.TT("dve", mT[:, fc, 0:N], acc[a][:, 0:N], tmp[x_][:, 0:N], ALU.add, [b_acc[a], b_tmp[x_]], [b_mT])
            for tt in range(ntile):
                t = g * 4 + tt
                xs = t % 2
                ws = 0
                it += 1
                if t + 1 < ntiles_total:
                    load_x(t + 1)
                halves = []
                for h in range(2):
                    ps, pb = self.ps()
                    for k in range(8):
                        self.MM(ps, mT[:, k, tt * 128:(tt + 1) * 128], wo[:, k, h * 512:(h + 1) * 512], k == 0, k == 7, [b_mT, b_wo], pb)
                    halves.append((ps, pb))
                t1, b_t1, y, b_y, st, b_st, ot, b_ot = work[ws]
                self.epilogue(halves, xt[xs], b_xt[xs], G if g < 8 else Gc, b_G if g < 8 else b_Gc, lng, lnb, b_ln, t1, b_t1, y, b_y, st, b_st, ot, b_ot)
                dst, db = self.xdst(layer, 1, t, False)
                self.DMA("sp", dst, ot, [b_ot], [db])
        P.barrier()

    def build(self):
        phases = []
        phases.append(("prologue", self.prologue))
        for i in range(self.L):
            last = i == self.L - 1
            need_ctx = not last
            phases.append(("L%d_ffn0a" % i, lambda i=i: self.ffn_a(i, 0)))
            phases.append(("L%d_ffn0b" % i, lambda i=i: self.ffn_b(i, 0)))
            phases.append(("L%d_mixp" % i, lambda i=i, nc_=need_ctx: self.mix_p(i, nc_)))
            phases.append(("L%d_att" % i, lambda i=i, nc_=need_ctx: self.att(i, nc_)))
            phases.append(("L%d_fft" % i, lambda i=i, nc_=need_ctx: self.fft(i, nc_)))
            phases.append(("L%d_merge" % i, lambda i=i, nc_=need_ctx: self.merge(i, nc_)))
            phases.append(("L%d_ffn1a" % i, lambda i=i: self.ffn_a(i, 1)))
            phases.append(("L%d_ffn1b" % i, lambda i=i, last=last: self.ffn_b(i, 1, ngroups=8 if last else 9, final=last)))
        for name, fn in phases:
            fn()
            if self.stop_after == name:
                break
        self.debug_dump()
        self.P.final_wait("sp")
        self.P.emit()
        return self.nc

    def debug_dump(self):
        if not self.dbg:
            return
        A = self.arena
        A.reset()
        for name, (ap, getter) in self.dbg.items():
            src = getter(self)
            R, Ccols = src.shape
            dt_ = src.dtype
            cw = 4096
            bufs = [A.alloc([128, cw], dt_) for _ in range(2)]
            bb = [Buf(), Buf()]
            it = 0
            for r0 in range(0, R, 128):
                rr = min(128, R - r0)
                for c0 in range(0, Ccols, cw):
                    cc = min(cw, Ccols - c0)
                    s_ = it % 2
                    it += 1
                    self.DMA("sp", bufs[s_][0:rr, 0:cc], src[r0:r0 + rr, c0:c0 + cc], [], [bb[s_]])
                    self.DMA("sp", ap[r0:r0 + rr, c0:c0 + cc], bufs[s_][0:rr, 0:cc], [bb[s_]], [Buf()])


_CACHE = {}


def _get_program():
    if "nc" not in _CACHE:
        _CACHE["nc"] = Builder(DEPTH).build()
        _CACHE["consts"] = make_consts()
    return _CACHE["nc"], _CACHE["consts"]


def kernel(x, c, ctx, c_ctx, w_mod, b_mod, w_ffn_up, w_ffn_down, ln_g, ln_b, w_in, attn_sink,
           sgu_w, sgu_b, sgu_ln_g, sgu_ln_b, w_gate, w_branch, w_out):
    nc, consts = _get_program()
    f32 = lambda a: np.ascontiguousarray(np.asarray(a, dtype=np.float32))
    shared = {"c_ctx": f32(c_ctx).reshape(1, D), "w_mod": f32(w_mod), "b_mod": f32(b_mod), "w_ffn_up": f32(w_ffn_up),
              "w_ffn_down": f32(w_ffn_down), "ln_g": f32(ln_g), "ln_b": f32(ln_b), "w_in": f32(w_in), "attn_sink": f32(attn_sink),
              "sgu_w": f32(sgu_w), "sgu_b": f32(sgu_b), "sgu_ln_g": f32(sgu_ln_g), "sgu_ln_b": f32(sgu_ln_b),
              "w_gate": f32(w_gate), "w_branch": f32(w_branch), "w_out": f32(w_out)}
    shared.update(consts)
    x = f32(x)
    c = f32(c)
    ctx = f32(ctx)
    in_maps = []
    for b in range(8):
        m = dict(shared)
        m["x"] = x[b]
        m["c"] = c[b].reshape(1, D)
        m["ctx"] = ctx[b]
        in_maps.append(m)
    res = run_bass_kernel_spmd(nc, in_maps, core_ids=list(range(8)))
    return np.stack([np.asarray(r["out"], dtype=np.float32) for r in res.results], axis=0)
```

```python
import contextlib
import math
import numpy as np
import ml_dtypes
import concourse.bass as bass
import concourse.mybir as mybir
from concourse.bass_utils import run_bass_kernel_spmd

F32 = mybir.dt.float32
BF16 = mybir.dt.bfloat16
U8 = mybir.dt.uint8
AF = mybir.ActivationFunctionType
ALU = mybir.AluOpType

D = 1024
SEQ = 4096
CTX = 256
T = SEQ + CTX
NT = T // 128
DEPTH = 4
DFF = 2752
NFC = 22
IN_W = 2304
ALPHA = (2 * DEPTH) ** 0.25
LN_EPS = 1e-5
GELU_C = 0.7978845608028654

ENGS = ("pe", "act", "dve", "pool", "sp")
SAME_ENGINE_SYNC = True


class Buf:
    __slots__ = ("name", "w", "r")

    def __init__(self, name=""):
        self.name = name
        self.w = []
        self.r = []


class Prog:
    N_DMA_SEMS = 40
    N_SW_SEMS = 8

    def __init__(self, nc):
        self.nc = nc
        self.ops = {e: [] for e in ENGS}
        self.cnt = {e: 0 for e in ENGS}
        self.seen = {e: {} for e in ENGS}
        self.dma_cnt = [0] * self.N_DMA_SEMS
        self.dma_next = 0
        self.sw_next = 0
        self.n_instr = 0

    def _collect(self, e, reads, writes, extra=(), is_dma=False):
        waits = {}

        def add(ev):
            k, v = ev
            if k == e and (e == "pe" or not SAME_ENGINE_SYNC):
                return
            if waits.get(k, 0) < v:
                waits[k] = v

        for b in reads:
            for ev in b.w:
                add(ev)
        for b in writes:
            for ev in b.w:
                add(ev)
            for ev in b.r:
                if ev[0] == e and not is_dma:
                    continue
                add(ev)
        for ev in extra:
            add(ev)
        seen = self.seen[e]
        for k, v in waits.items():
            if seen.get(k, 0) < v:
                seen[k] = v
                self.ops[e].append(("wait", k, v))

    @staticmethod
    def _compress(evs):
        d = {}
        for k, v in evs:
            if d.get(k, 0) < v:
                d[k] = v
        return list(d.items())

    def _record(self, ev, reads, writes):
        for b in reads:
            b.r.append(ev)
            if len(b.r) > 48:
                b.r = self._compress(b.r)
        for b in writes:
            b.w = [ev]
            b.r = []

    def op(self, e, fn, reads=(), writes=()):
        self._collect(e, reads, writes)
        self.cnt[e] += 1
        ev = (e, self.cnt[e])
        self.ops[e].append(("op", fn))
        self._record(ev, reads, writes)
        self.n_instr += 1
        return ev

    def mm(self, fn, reads, psum, first):
        self._collect("pe", reads, [psum] if first else [])
        self.cnt["pe"] += 1
        ev = ("pe", self.cnt["pe"])
        self.ops["pe"].append(("op", fn))
        for b in reads:
            b.r.append(ev)
            if len(b.r) > 48:
                b.r = self._compress(b.r)
        if first:
            psum.r = []
        psum.w = [ev]
        self.n_instr += 1
        return ev

    def dma(self, e, fn, reads=(), writes=()):
        if e == "pool":
            j = self.N_DMA_SEMS - self.N_SW_SEMS + self.sw_next
            self.sw_next = (self.sw_next + 1) % self.N_SW_SEMS
        else:
            j = self.dma_next
            self.dma_next = (self.dma_next + 1) % (self.N_DMA_SEMS - self.N_SW_SEMS)
        key = ("dma", j)
        prev = self.dma_cnt[j]
        extra = [(key, prev)] if prev else []
        self._collect(e, reads, writes, extra, is_dma=True)
        self.dma_cnt[j] += 16
        ev = (key, self.dma_cnt[j])
        self.ops[e].append(("dma", fn, key))
        self._record(ev, reads, writes)
        self.n_instr += 1
        return ev

    def gate_swdge(self, e="dve"):
        seen = self.seen[e]
        for j in range(self.N_DMA_SEMS - self.N_SW_SEMS, self.N_DMA_SEMS):
            k, v = ("dma", j), self.dma_cnt[j]
            if v and seen.get(k, 0) < v:
                seen[k] = v
                self.ops[e].append(("wait", k, v))

    def _all_events(self):
        evs = [(e, self.cnt[e]) for e in ENGS if self.cnt[e]]
        evs += [(("dma", j), c) for j, c in enumerate(self.dma_cnt) if c]
        return evs

    def barrier(self):
        evs = self._all_events()
        for e in ENGS:
            seen = self.seen[e]
            for k, v in evs:
                if k == e and e == "pe":
                    continue
                if seen.get(k, 0) < v:
                    seen[k] = v
                    self.ops[e].append(("wait", k, v))

    def final_wait(self, e="sp"):
        seen = self.seen[e]
        for k, v in self._all_events():
            if seen.get(k, 0) < v:
                seen[k] = v
                self.ops[e].append(("wait", k, v))

    def emit(self):
        nc = self.nc
        with contextlib.ExitStack() as st:
            sems = {}
            for e in ENGS:
                sems[e] = st.enter_context(nc.semaphore("s_" + e))
            for j in range(self.N_DMA_SEMS):
                sems[("dma", j)] = st.enter_context(nc.semaphore("s_dma%d" % j))
            block = st.enter_context(nc.Block())

            def run(eng, e):
                mysem = sems[e]
                for item in self.ops[e]:
                    t = item[0]
                    if t == "wait":
                        eng.wait_ge(sems[item[1]], item[2])
                    elif t == "op":
                        item[1](eng).then_inc(mysem, 1)
                    else:
                        item[1](eng).then_inc(sems[item[2]], 16)

            @block.tensor
            def _(eng):
                run(eng, "pe")

            @block.scalar
            def _(eng):
                run(eng, "act")

            @block.vector
            def _(eng):
                run(eng, "dve")

            @block.gpsimd
            def _(eng):
                run(eng, "pool")

            @block.sync
            def _(eng):
                run(eng, "sp")


def _dtsize(dt):
    return 4 if dt == F32 else 2


class Arena:
    def __init__(self, ap_u8, size):
        self.ap = ap_u8
        self.size = size
        self.off = 0

    def reset(self):
        self.off = 0

    def alloc(self, shape, dt):
        n = 1
        for s in shape[1:]:
            n *= s
        nbytes = n * _dtsize(dt)
        off = self.off
        self.off += (nbytes + 63) // 64 * 64
        assert self.off <= self.size, "arena overflow %d > %d" % (self.off, self.size)
        a = self.ap[0:shape[0], off:off + nbytes].bitcast(dt)
        if len(shape) == 3:
            a = a.rearrange("p (a b) -> p a b", a=shape[1])
        elif len(shape) == 4:
            a = a.rearrange("p (a b c) -> p a b c", a=shape[1], b=shape[2])
        elif len(shape) == 5:
            a = a.rearrange("p (a b c d) -> p a b c d", a=shape[1], b=shape[2], c=shape[3])
        return a


def _bf(a):
    return np.asarray(a, dtype=np.float32).astype(ml_dtypes.bfloat16)


def make_consts():
    c = {}
    c["ident_f"] = np.eye(128, dtype=np.float32)
    c["ident_b"] = _bf(np.eye(128))
    c["ones_b"] = _bf(np.ones((128, 64)))
    c["ones_f"] = np.ones((1, 128), dtype=np.float32)
    tok = np.arange(SEQ)
    row = (tok // 64).astype(np.float32)
    col = (tok % 64).astype(np.float32)
    inv_freq = (np.float32(10000.0) ** (-np.arange(0, 32, 2, dtype=np.float32) / np.float32(32))).astype(np.float32)
    cosT = np.zeros((128, SEQ), np.float32)
    sinT = np.zeros((128, SEQ), np.float32)
    for p in range(128):
        d = p % 64
        pos = row if d < 32 else col
        fi = (d % 32) % 16
        ang = (pos * inv_freq[fi]).astype(np.float32)
        cosT[p] = np.cos(ang)
        s = np.sin(ang)
        sinT[p] = -s if (d % 32) < 16 else s
    c["ropec"] = cosT
    c["ropes"] = sinT
    j = np.arange(128)[:, None]
    i = np.arange(128)[None, :]
    mp = np.where(j < i, -30000.0, 0.0)
    mn = np.where(j > i, -30000.0, 0.0)
    c["maskp"] = _bf(np.tile(mp, (1, 4)))
    c["maskn"] = _bf(np.tile(mn, (1, 4)))
    n = np.arange(128, dtype=np.float64)
    a128 = 2 * np.pi * np.outer(n, n) / 128.0
    c["c128"] = _bf(np.cos(a128) / np.sqrt(128.0))
    c["s128n"] = _bf(-np.sin(a128) / np.sqrt(128.0))
    c["s128"] = _bf(np.sin(a128) / np.sqrt(128.0))
    k1 = np.arange(128, dtype=np.float64)[:, None]
    n2 = np.arange(32, dtype=np.float64)[None, :]
    atw = 2 * np.pi * k1 * n2 / 4096.0
    c["tw"] = np.concatenate([np.cos(atw), np.sin(atw)], axis=1).astype(np.float32)
    n1 = np.arange(128, dtype=np.float64)[:, None, None]
    n2_ = np.arange(32, dtype=np.float64)[None, :, None]
    k1_ = np.arange(128, dtype=np.float64)[None, None, :]
    ang = 2 * np.pi * k1_ * (32 * n1 + n2_) / 4096.0
    c["tw128"] = _bf(np.stack([np.cos(ang), -np.sin(ang)], axis=2) / np.sqrt(128.0))
    w2 = np.zeros((128, 128), np.float64)
    for k1lo in range(2):
        for ri in range(2):
            for nn in range(32):
                r_ = k1lo * 64 + ri * 32 + nn
                for rix in range(2):
                    for k2 in range(32):
                        c_ = k1lo * 64 + rix * 32 + k2
                        ang = 2 * np.pi * k2 * nn / 32.0
                        cs, sn = np.cos(ang), np.sin(ang)
                        if ri == 0:
                            v = cs if rix == 0 else -sn
                        else:
                            v = sn if rix == 0 else cs
                        w2[r_, c_] = v / np.sqrt(32.0)
    c["w2"] = _bf(w2)
    c["ccscn"] = _bf(np.concatenate([np.cos(a128), -np.sin(a128)], axis=1) / np.sqrt(128.0))
    nn = np.arange(256, dtype=np.float64)
    a256 = 2 * np.pi * np.outer(nn, nn) / 256.0
    c["c256"] = _bf((np.cos(a256) / 16.0).reshape(2, 128, 256).transpose(1, 0, 2))
    c["s256"] = _bf((np.sin(a256) / 16.0).reshape(2, 128, 256).transpose(1, 0, 2))
    return c


CONST_SPECS = {
    "ident_f": ([128, 128], F32), "ident_b": ([128, 128], BF16), "ones_b": ([128, 64], BF16), "ones_f": ([1, 128], F32),
    "ropec": ([128, SEQ], F32), "ropes": ([128, SEQ], F32), "maskp": ([128, 512], BF16), "maskn": ([128, 512], BF16),
    "c128": ([128, 128], BF16), "s128n": ([128, 128], BF16), "s128": ([128, 128], BF16), "tw": ([128, 64], F32),
    "w2": ([128, 128], BF16), "tw128": ([128, 32, 2, 128], BF16), "ccscn": ([128, 256], BF16), "c256": ([128, 2, 256], BF16), "s256": ([128, 2, 256], BF16),
}

WEIGHT_SPECS = {
    "w_mod": [D, 9 * D], "b_mod": [9 * D], "w_ffn_up": [2, D, 2 * DFF], "w_ffn_down": [2, DFF, D],
    "ln_g": [3, D], "ln_b": [3, D], "w_in": [D, IN_W], "attn_sink": [8], "sgu_w": [4, 128, 128], "sgu_b": [4, 128],
    "sgu_ln_g": [512], "sgu_ln_b": [512], "w_gate": [3, D, D], "w_branch": [3, 512, D], "w_out": [D, D],
}


class Builder:
    def __init__(self, L=DEPTH, stop_after=None, debug_out=(), wl=None):
        self.L = L
        wl = wl or L
        self.wl = wl
        self.stop_after = stop_after
        nc = self.nc = bass.Bass("TRN2", target_bir_lowering=False)
        self.P = Prog(nc)
        P = self.P
        dt = nc.dram_tensor
        self.x_in = dt("x", [SEQ, D], F32, kind="ExternalInput").ap()
        self.c_in = dt("c", [1, D], F32, kind="ExternalInput").ap()
        self.ctx_in = dt("ctx", [CTX, D], F32, kind="ExternalInput").ap()
        self.cctx_in = dt("c_ctx", [1, D], F32, kind="ExternalInput").ap()
        self.W = {}
        for k, shp in WEIGHT_SPECS.items():
            self.W[k] = dt(k, [wl] + shp, F32, kind="ExternalInput").ap()
        self.C = {}
        for k, (shp, d_) in CONST_SPECS.items():
            self.C[k] = dt(k, shp, d_, kind="ExternalInput").ap()
        self.out = dt("out", [SEQ, D], F32, kind="ExternalOutput").ap()
        self.x_d = dt("x_d", [T, D], F32).ap()
        self.mods_d = dt("mods_d", [L, 2, 9 * D], F32).ap()
        self.aT_d = dt("aT_d", [NFC, 128, T], BF16).ap()
        self.hT_d = dt("hT_d", [8, 128, T], BF16).ap()
        self.qT_d = dt("qT_d", [8, 64, T], BF16).ap()
        self.kT_d = dt("kT_d", [2, 64, T], BF16).ap()
        self.v_d = dt("v_d", [T, 128], BF16).ap()
        self.f_d = dt("f_d", [SEQ, 512], BF16).ap()
        self.fcT_d = dt("fcT_d", [4, 128, CTX], BF16).ap()
        self.brT_d = [dt("attT_d", [8, 64, T], BF16).ap(), dt("sguT_d", [4, 128, T], BF16).ap(), dt("fouT_d", [4, 128, T], BF16).ap()]
        self.B_d = dt("B_d", [128, 2, 32, 512], BF16).ap()
        self.dbg = {}
        for name, shp, d_, getter in debug_out:
            self.dbg[name] = (dt(name, shp, d_, kind="ExternalOutput").ap(), getter)
        self.b_xsrc = [Buf("xin%d" % t) for t in range(NT)]
        self.b_xd = [Buf("xd%d" % t) for t in range(NT)]
        self.b_mods = [Buf("mods_d%d" % i) for i in range(L)]
        self.b_aT = [Buf("aT_d%d" % g) for g in range(9)]
        self.b_hT = [Buf("hT_d%d" % g) for g in range(9)]
        self.b_q = [Buf() for g in range(9)]
        self.b_k = [Buf() for g in range(9)]
        self.b_v = [Buf() for g in range(9)]
        self.b_f = [Buf() for g in range(8)]
        self.b_fc = Buf()
        self.b_br = [[Buf() for g in range(9)] for r in range(3)]
        self.b_Bd = [Buf() for g in range(8)]
        self.b_const = Buf("const")
        self.b_w = Buf("weights")

        sb = nc.alloc_sbuf_tensor
        self.ident_f = sb("sb_ident_f", [128, 128], F32).ap()
        self.ident_b = sb("sb_ident_b", [128, 128], BF16).ap()
        self.ones_b = sb("sb_ones_b", [128, 64], BF16).ap()
        self.ones_f = sb("sb_ones_f", [1, 128], F32).ap()
        self.modT = sb("modT", [128, L, 2, 72], F32).ap()
        self.eps_t = sb("eps_t", [128, 1], F32).ap()
        self.b_pc = Buf("persist_consts")
        self.b_modT = [Buf("modT%d" % i) for i in range(L)]
        self.sT = sb("sT", [128, 8, 2], F32).ap()
        self.b_sT = Buf("sT")
        rem = nc.sbuf_bytes_remaining - 256
        self.arena_size = rem // 64 * 64
        self.arena = Arena(sb("arena", [128, self.arena_size], U8).ap(), self.arena_size)
        self.psum = [nc.alloc_psum_tensor("ps%d" % i, [128, 512], F32).ap() for i in range(8)]
        self.b_ps = [Buf("ps%d" % i) for i in range(8)]
        self.ps_next = 0

        for nm, dst in (("ident_f", self.ident_f), ("ident_b", self.ident_b), ("ones_b", self.ones_b), ("ones_f", self.ones_f)):
            self.DMA("sp", dst, self.C[nm], [self.b_const], [self.b_pc])
        P.op("dve", lambda e: e.memset(self.eps_t, LN_EPS), writes=[self.b_pc])

    def ps(self):
        i = self.ps_next
        self.ps_next = (i + 1) % 8
        return self.psum[i], self.b_ps[i]

    def DMA(self, q, out, in_, reads, writes, **kw):
        self.P.dma(q, lambda e: e.dma_start(out=out, in_=in_, **kw), reads, writes)

    def MM(self, out, lhsT, rhs, start, stop, reads, psb):
        self.P.mm(lambda e: e.matmul(out, lhsT=lhsT, rhs=rhs, start=start, stop=stop), reads, psb, first=start)

    def TR(self, out, in_, ident, reads, psb, first):
        self.P.mm(lambda e: e.transpose(out, in_, ident), reads, psb, first=first)

    def ACT(self, out, in_, func, reads, writes, scale=None, bias=None, accum_out=None):
        kw = {}
        if scale is not None:
            kw["scale"] = scale
        if bias is not None:
            kw["bias"] = bias
        if accum_out is not None:
            kw["accum_out"] = accum_out
        self.P.op("act", lambda e: e.activation(out=out, in_=in_, func=func, **kw), reads, writes)

    def TT(self, eng, out, in0, in1, op, reads, writes):
        self.P.op(eng, lambda e: e.tensor_tensor(out=out, in0=in0, in1=in1, op=op), reads, writes)

    def TS(self, eng, out, in0, s1, op0, reads, writes, s2=None, op1=None):
        if op1 is None:
            self.P.op(eng, lambda e: e.tensor_scalar(out=out, in0=in0, scalar1=s1, scalar2=None, op0=op0), reads, writes)
        else:
            self.P.op(eng, lambda e: e.tensor_scalar(out=out, in0=in0, scalar1=s1, scalar2=s2, op0=op0, op1=op1), reads, writes)

    def STT(self, out, in0, scalar, in1, op0, op1, reads, writes):
        self.P.op("dve", lambda e: e.scalar_tensor_tensor(out=out, in0=in0, scalar=scalar, in1=in1, op0=op0, op1=op1), reads, writes)

    def CP(self, eng, out, in_, reads, writes):
        if eng == "act":
            self.P.op(eng, lambda e: e.activation(out=out, in_=in_, func=AF.Copy), reads, writes)
        else:
            self.P.op(eng, lambda e: e.tensor_copy(out=out, in_=in_), reads, writes)

    def mcol(self, i, jl, m, k):
        c0 = m * 8 + k
        return self.modT[:, i, jl, c0:c0 + 1]

    def tile_rows(self, ap, t):
        return ap[t * 128:(t + 1) * 128, :]

    def xsrc(self, layer, j, t):
        if layer == 0 and j == 0:
            if t < 32:
                return self.x_in[t * 128:(t + 1) * 128, :], self.b_xsrc[t]
            return self.ctx_in[(t - 32) * 128:(t - 31) * 128, :], self.b_xsrc[t]
        return self.x_d[t * 128:(t + 1) * 128, :], self.b_xd[t]

    def prologue(self):
        P, A = self.P, self.arena
        A.reset()
        sT = self.sT
        b_sT = self.b_sT
        self.DMA("sp", sT[:, :, 0], self.c_in.rearrange("o (k p) -> p (o k)", p=128), [self.b_const], [b_sT], allow_slow_non_contiguous=True)
        self.DMA("sp", sT[:, :, 1], self.cctx_in.rearrange("o (k p) -> p (o k)", p=128), [self.b_const], [b_sT], allow_slow_non_contiguous=True)
        sT2 = sT.rearrange("p k j -> p (k j)")
        self.ACT(sT2, sT2, AF.Silu, [b_sT], [b_sT])
        for st_ in self.mods_job(A, 0):
            st_()
        P.barrier()

    def mods_job(self, A, i):
        NB = 3
        sT, b_sT = self.sT, self.b_sT
        wm = [A.alloc([128, 8, 512], F32) for _ in range(NB)]
        b_wm = [Buf("wm%d" % k) for k in range(NB)]
        bm = [A.alloc([2, 512], F32) for _ in range(NB)]
        b_bm = [Buf() for _ in range(NB)]
        mrow = [A.alloc([2, 512], F32) for _ in range(NB)]
        b_mrow = [Buf() for _ in range(NB)]

        def load_wm(n):
            s = n % NB
            self.DMA("sp", wm[s], self.W["w_mod"][i, :, n * 512:(n + 1) * 512].rearrange("(k p) n -> p k n", p=128), [self.b_w], [b_wm[s]])
            self.DMA("sp", bm[s], self.W["b_mod"][i:i + 1, n * 512:(n + 1) * 512].broadcast_to([2, 512]), [self.b_w], [b_bm[s]])

        def make_step(n):
            def step():
                s = n % NB
                if n == 0:
                    load_wm(0)
                    load_wm(1)
                if n + 2 < 18:
                    load_wm(n + 2)
                ps, pb = self.ps()
                for k in range(8):
                    self.MM(ps[0:2, :], sT[:, k, :], wm[s][:, k, :], k == 0, k == 7, [b_sT, b_wm[s]], pb)
                self.TT("dve", mrow[s], ps[0:2, :], bm[s], ALU.add, [pb, b_bm[s]], [b_mrow[s]])
                self.DMA("sp", self.mods_d[i, :, n * 512:(n + 1) * 512], mrow[s], [b_mrow[s]], [self.b_mods[i]])
                pt, ptb = self.ps()
                for j in range(4):
                    self.TR(pt[:, j * 2:(j + 1) * 2], mrow[s][0:2, j * 128:(j + 1) * 128], self.ident_f[0:2, 0:2], [b_mrow[s], self.b_pc], ptb, j == 0)
                m = n // 2
                c0 = m * 8 + (n % 2) * 4
                self.CP("dve", self.modT[:, i, :, c0:c0 + 4], pt[:, 0:8].rearrange("p (j l) -> p l j", l=2), [ptb], [self.b_modT[i]])
                if n % 2 == 1 and m in (1, 4, 7):
                    v = self.modT[:, i, :, m * 8:(m + 1) * 8]
                    self.TS("dve", v, v, 1.0, ALU.add, [self.b_modT[i]], [self.b_modT[i]])
            return step
        return [make_step(n) for n in range(18)]

    def load_xg(self, layer, j, g, xg, b_xg):
        ntile = 4 if g < 8 else 2
        for tt in range(ntile):
            src, sb_ = self.xsrc(layer, j, g * 4 + tt)
            self.DMA("sp", xg[:, tt, :], src, [sb_], [b_xg[tt]])

    def make_hT(self, layer, j, g, ntile, xg, b_xg, hT, b_hT, m_shift, m_scale, jl):
        N = ntile * 128
        for kq in range(2):
            banks = [self.ps() for _ in range(4)]
            for kk in range(4):
                k = kq * 4 + kk
                ps, pb = banks[kk]
                for tt in range(ntile):
                    self.TR(ps[:, tt * 128:(tt + 1) * 128], xg[:, tt, k * 128:(k + 1) * 128], self.ident_f, [b_xg[tt], self.b_pc], pb, tt == 0)
            for kk in range(4):
                k = kq * 4 + kk
                ps, pb = banks[kk]
                sc = self.mcol(layer, jl, m_scale, k)
                sh = self.mcol(layer, jl, m_shift, k)
                if kk % 2 == 0:
                    self.ACT(hT[:, k, 0:N], ps[:, 0:N], AF.Identity, [pb, self.b_modT[layer]], [b_hT], scale=sc, bias=sh)
                else:
                    self.TS("dve", hT[:, k, 0:N], ps[:, 0:N], sc, ALU.mult, [pb, self.b_modT[layer]], [b_hT], s2=sh, op1=ALU.add)

    def load_w_cast(self, dst, src, wbuf, **kw):
        self.DMA("pool", dst, src, [self.b_w], [wbuf], **kw)

    def ffn_a(self, layer, j):
        P, A = self.P, self.arena
        A.reset()
        mb = 0 if j == 0 else 6
        wu = A.alloc([128, 8, 2 * DFF], BF16)
        b_wu = [Buf("wu%d" % k) for k in range(4)]
        wsrc = self.W["w_ffn_up"][layer, j].rearrange("(k p) n -> p k n", p=128)
        bounds = [0, 6, 12, 17, 22]
        self.wu_part = []
        for q in range(4):
            c0, c1 = bounds[q] * 128, min(bounds[q + 1] * 128, DFF)
            for base in (0, DFF):
                self.load_w_cast(wu[:, :, base + c0:base + c1], wsrc[:, :, base + c0:base + c1], b_wu[q], max_dma_last_dim=8192)
        P.gate_swdge()

        def wbuf_of(c):
            for q in range(4):
                if bounds[q] <= c < bounds[q + 1]:
                    return b_wu[q]
        xg = [A.alloc([128, 4, D], F32) for _ in range(2)]
        b_xg = [[Buf() for _ in range(4)] for _ in range(2)]
        hT = [A.alloc([128, 8, 512], BF16) for _ in range(2)]
        b_hT = [Buf(), Buf()]
        aT = [A.alloc([128, NFC, 512], BF16) for _ in range(2)]
        b_aT = [Buf(), Buf()]
        sg = [A.alloc([128, 512], F32) for _ in range(2)]
        b_sg = [Buf(), Buf()]
        for g in range(9):
            ntile = 4 if g < 8 else 2
            N = ntile * 128
            jl = 0 if g < 8 else 1
            s = g % 2
            if g == 0:
                self.load_xg(layer, j, 0, xg[0], b_xg[0])
            if g + 1 < 9:
                self.load_xg(layer, j, g + 1, xg[1 - s], b_xg[1 - s])
            self.make_hT(layer, j, g, ntile, xg[s], b_xg[s], hT[s], b_hT[s], mb + 0, mb + 1, jl)
            for c in range(NFC):
                wc = 128 if c < NFC - 1 else 64
                pg, pgb = self.ps()
                pu, pub = self.ps()
                wb_ = wbuf_of(c)
                for k in range(8):
                    self.MM(pg[0:wc, 0:N], wu[:, k, c * 128:c * 128 + wc], hT[s][:, k, 0:N], k == 0, k == 7, [wb_, b_hT[s]], pgb)
                for k in range(8):
                    self.MM(pu[0:wc, 0:N], wu[:, k, DFF + c * 128:DFF + c * 128 + wc], hT[s][:, k, 0:N], k == 0, k == 7, [wb_, b_hT[s]], pub)
                ss = c % 2
                self.ACT(sg[ss][0:wc, 0:N], pg[0:wc, 0:N], AF.Silu, [pgb], [b_sg[ss]])
                self.TT("dve", aT[s][0:wc, c, 0:N], sg[ss][0:wc, 0:N], pu[0:wc, 0:N], ALU.mult, [b_sg[ss], pub], [b_aT[s]])
            t0 = g * 512
            self.DMA("sp", self.aT_d[0:NFC - 1, :, t0:t0 + N].rearrange("c p t -> p c t"), aT[s][:, 0:NFC - 1, 0:N], [b_aT[s]], [self.b_aT[g]])
            self.DMA("sp", self.aT_d[NFC - 1, 0:64, t0:t0 + N], aT[s][0:64, NFC - 1, 0:N], [b_aT[s]], [self.b_aT[g]])
        P.barrier()

    def epilogue_multi(self, tiles, lng, lnb, b_ln):
        for tl in tiles:
            t1, b_t1, y, b_y, st, b_st, _, _ = tl["work"]
            for h in range(2):
                ps, pb = tl["halves"][h]
                self.TT("dve", t1[:, h * 512:(h + 1) * 512], ps, tl["G"][:, h * 512:(h + 1) * 512], ALU.mult, [pb, tl["b_G"]], [b_t1])
        for tl in tiles:
            t1, b_t1, y, b_y, st, b_st, _, _ = tl["work"]
            self.STT(y, tl["xt"], ALPHA, t1, ALU.mult, ALU.add, [tl["b_xt"], b_t1], [b_y])
        for tl in tiles:
            t1, b_t1, y, b_y, st, b_st, _, _ = tl["work"]
            stats, mv, sd, rstd, nb = st
            for h in range(2):
                self.P.op("dve", lambda e, h=h, stats=stats, y=y: e.bn_stats(out=stats[:, h * 6:(h + 1) * 6], in_=y[:, h * 512:(h + 1) * 512]), [b_y], [b_st])
            self.P.op("dve", lambda e, stats=stats, mv=mv: e.bn_aggr(out=mv, in_=stats), [b_st], [b_st])
        for tl in tiles:
            t1, b_t1, y, b_y, st, b_st, _, _ = tl["work"]
            stats, mv, sd, rstd, nb = st
            self.ACT(sd, mv[:, 1:2], AF.Sqrt, [b_st, self.b_pc], [b_st], bias=self.eps_t[:, 0:1])
        for tl in tiles:
            t1, b_t1, y, b_y, st, b_st, _, _ = tl["work"]
            stats, mv, sd, rstd, nb = st
            self.P.op("dve", lambda e, rstd=rstd, sd=sd: e.reciprocal(out=rstd, in_=sd), [b_st], [b_st])
            self.TS("dve", nb, mv[:, 0:1], -1.0, ALU.mult, [b_st], [b_st], s2=rstd[:, 0:1], op1=ALU.mult)
        for tl in tiles:
            t1, b_t1, y, b_y, st, b_st, _, _ = tl["work"]
            stats, mv, sd, rstd, nb = st
            self.ACT(t1, y, AF.Identity, [b_y, b_st], [b_t1], scale=rstd[:, 0:1], bias=nb[:, 0:1])
        for tl in tiles:
            t1, b_t1, y, b_y, st, b_st, _, _ = tl["work"]
            self.TT("dve", y, t1, lng, ALU.mult, [b_t1, b_ln], [b_y])
            self.TT("dve", y, y, lnb, ALU.add, [b_y, b_ln], [b_y])
            self.DMA("sp", tl["dst"], y, [b_y], [tl["b_dst"]])

    def load_epi_consts(self, A, layer, gate_m, lnidx, half):
        G = A.alloc([128, D], F32)
        Gc = A.alloc([128, D], F32)
        lng = A.alloc([128, D], F32)
        lnb = A.alloc([128, D], F32)
        b_G, b_Gc, b_ln = Buf(), Buf(), Buf()
        self.DMA("sp", G, self.mods_d[layer, 0:1, gate_m * D:(gate_m + 1) * D].broadcast_to([128, D]), [self.b_mods[layer]], [b_G])
        self.DMA("sp", Gc, self.mods_d[layer, 1:2, gate_m * D:(gate_m + 1) * D].broadcast_to([128, D]), [self.b_mods[layer]], [b_Gc])
        self.DMA("sp", lng, self.W["ln_g"][layer, lnidx:lnidx + 1, :].broadcast_to([128, D]), [self.b_w], [b_ln])
        self.DMA("sp", lnb, self.W["ln_b"][layer, lnidx:lnidx + 1, :].broadcast_to([128, D]), [self.b_w], [b_ln])
        if half:
            self.P.op("act", lambda e: e.mul(out=G, in_=G, mul=0.5), [b_G], [b_G])
            self.P.op("act", lambda e: e.mul(out=Gc, in_=Gc, mul=0.5), [b_Gc], [b_Gc])
        return G, b_G, Gc, b_Gc, lng, lnb, b_ln

    def alloc_epi_work(self, A, n):
        work = []
        for _ in range(n):
            t1 = A.alloc([128, D], F32)
            y = A.alloc([128, D], F32)
            st = (A.alloc([128, 12], F32), A.alloc([128, 2], F32), A.alloc([128, 1], F32), A.alloc([128, 1], F32), A.alloc([128, 1], F32))
            b_y_ = Buf()
            work.append((t1, Buf(), y, b_y_, st, Buf(), y, b_y_))
        return work

    def xdst(self, layer, j, t, final):
        if final and t < 32:
            return self.out[t * 128:(t + 1) * 128, :], self.b_xd[t]
        return self.x_d[t * 128:(t + 1) * 128, :], self.b_xd[t]

    def ffn_b(self, layer, j, ngroups=9, final=False):
        P, A = self.P, self.arena
        A.reset()
        mb = 0 if j == 0 else 6
        wd = A.alloc([128, NFC, D], BF16)
        b_wd = Buf("wd")
        wsrc = self.W["w_ffn_down"][layer, j]
        self.load_w_cast(wd[:, 0:NFC - 1, :], wsrc[0:(NFC - 1) * 128, :].rearrange("(k p) n -> p k n", p=128), b_wd)
        self.load_w_cast(wd[0:64, NFC - 1, :], wsrc[(NFC - 1) * 128:DFF, :], b_wd)
        P.gate_swdge()
        G, b_G, Gc, b_Gc, lng, lnb, b_ln = self.load_epi_consts(A, layer, mb + 2, 0 if j == 0 else 2, True)
        aTs = [A.alloc([128, NFC, 512], BF16) for _ in range(2)]
        b_aTs = [Buf(), Buf()]
        NX = 4
        xt = [A.alloc([128, D], F32) for _ in range(NX)]
        b_xt = [Buf() for _ in range(NX)]
        NWK = 4
        work = self.alloc_epi_work(A, NWK)
        ntiles_total = sum(4 if g < 8 else 2 for g in range(ngroups))

        def load_aT(g):
            ntile = 4 if g < 8 else 2
            N = ntile * 128
            s = g % 2
            t0 = g * 512
            self.DMA("sp", aTs[s][:, 0:NFC - 1, 0:N], self.aT_d[0:NFC - 1, :, t0:t0 + N].rearrange("c p t -> p c t"), [self.b_aT[g]], [b_aTs[s]])
            self.DMA("sp", aTs[s][0:64, NFC - 1, 0:N], self.aT_d[NFC - 1, 0:64, t0:t0 + N], [self.b_aT[g]], [b_aTs[s]])

        def load_x(t):
            src, sb_ = self.xsrc(layer, j, t)
            self.DMA("sp", xt[t % NX], src, [sb_], [b_xt[t % NX]])

        load_aT(0)
        load_x(0)
        load_x(1)
        it = 0
        for g in range(ngroups):
            ntile = 4 if g < 8 else 2
            N = ntile * 128
            s = g % 2
            t0 = g * 512
            if g + 1 < ngroups:
                load_aT(g + 1)
            for pr in range(ntile // 2):
                tiles = []
                for tt in (2 * pr, 2 * pr + 1):
                    t = g * 4 + tt
                    xs = t % NX
                    ws = it % NWK
                    it += 1
                    if t + 2 < ntiles_total:
                        load_x(t + 2)
                    halves = []
                    for h in range(2):
                        ps, pb = self.ps()
                        for c in range(NFC):
                            kk = 128 if c < NFC - 1 else 64
                            self.MM(ps, aTs[s][0:kk, c, tt * 128:(tt + 1) * 128], wd[0:kk, c, h * 512:(h + 1) * 512], c == 0, c == NFC - 1, [b_aTs[s], b_wd], pb)
                        halves.append((ps, pb))
                    dst, db = self.xdst(layer, j, t, final)
                    tiles.append(dict(halves=halves, xt=xt[xs], b_xt=b_xt[xs], G=G if g < 8 else Gc, b_G=b_G if g < 8 else b_Gc,
                                      work=work[ws], dst=dst, b_dst=db))
                self.epilogue_multi(tiles, lng, lnb, b_ln)
        P.barrier()

    def mix_p(self, layer, need_ctx):
        P, A = self.P, self.arena
        A.reset()
        W = self.W
        win = A.alloc([128, 8, IN_W], BF16)
        b_win = Buf("win")
        wsrc = W["w_in"][layer].rearrange("(k p) n -> p k n", p=128)
        self.load_w_cast(win[:, :, 0:1152], wsrc[:, :, 0:1152], b_win)
        self.load_w_cast(win[:, :, 1152:IN_W], wsrc[:, :, 1152:IN_W], b_win)
        P.gate_swdge()
        wperm = A.alloc([128, 8, 640], BF16)
        b_wperm = Buf("wperm")
        wv = win[:, :, 0:640].rearrange("p k (h s e) -> p k h s e", s=2, e=16)
        pv = wperm.rearrange("p k (h s e) -> p k h s e", s=2, e=16)
        for k in range(8):
            self.CP("act", pv[:, k, :, 0, :], wv[:, k, :, 1, :], [b_win], [b_wperm])
            self.CP("act", pv[:, k, :, 1, :], wv[:, k, :, 0, :], [b_win], [b_wperm])
        wsn = A.alloc([128, 4, 128], F32)
        b_wsn = Buf()
        self.DMA("sp", wsn, W["sgu_w"][layer].rearrange("g p q -> p g q"), [self.b_w], [b_wsn])
        wsT = A.alloc([128, 4, 128], BF16)
        b_wsT = Buf()
        ps, pb = self.ps()
        for gq in range(4):
            self.TR(ps[:, gq * 128:(gq + 1) * 128], wsn[:, gq, :], self.ident_f, [b_wsn, self.b_pc], pb, gq == 0)
        self.CP("dve", wsT.rearrange("p g q -> p (g q)"), ps, [pb], [b_wsT])
        sbr = A.alloc([1, 4, 128], F32)
        b_sbr = Buf()
        self.DMA("sp", sbr, W["sgu_b"][layer:layer + 1], [self.b_w], [b_sbr])
        slg = A.alloc([128, 512], F32)
        slb = A.alloc([128, 512], F32)
        b_sl = Buf()
        self.DMA("sp", slg, W["sgu_ln_g"][layer:layer + 1, :].broadcast_to([128, 512]), [self.b_w], [b_sl])
        self.DMA("sp", slb, W["sgu_ln_b"][layer:layer + 1, :].broadcast_to([128, 512]), [self.b_w], [b_sl])

        xg = [A.alloc([128, 4, D], F32) for _ in range(2)]
        b_xg = [[Buf() for _ in range(4)] for _ in range(2)]
        hT = [A.alloc([128, 8, 512], BF16) for _ in range(2)]
        b_hT = [Buf(), Buf()]
        rc = [A.alloc([128, 512], F32) for _ in range(2)]
        rs = [A.alloc([128, 512], F32) for _ in range(2)]
        b_rt = [Buf(), Buf()]
        qk = [A.alloc([128, 5, 512], BF16) for _ in range(2)]
        b_qk = [Buf(), Buf()]
        r1 = [A.alloc([128, 512], F32) for _ in range(2)]
        r2 = [A.alloc([128, 512], F32) for _ in range(2)]
        b_r = [Buf(), Buf()]
        uT = [A.alloc([128, 4, 512], F32) for _ in range(2)]
        b_uT = [Buf(), Buf()]
        gtmp = [A.alloc([128, 512], F32) for _ in range(2)]
        b_gt = [Buf(), Buf()]
        sgT = [A.alloc([128, 4, 512], BF16) for _ in range(2)]
        b_sgT = [Buf(), Buf()]
        vt = [A.alloc([128, 4, 128], BF16) for _ in range(2)]
        b_vt = [Buf(), Buf()]
        ft = [A.alloc([128, 4, 512], BF16) for _ in range(2)]
        b_ft = [Buf(), Buf()]
        NZ = 4
        zt = [A.alloc([128, 512], F32) for _ in range(NZ)]
        zg = [A.alloc([128, 512], F32) for _ in range(NZ)]
        zn = [A.alloc([128, 512], BF16) for _ in range(NZ)]
        zst = [(A.alloc([128, 6], F32), A.alloc([128, 2], F32), A.alloc([128, 1], F32), A.alloc([128, 1], F32), A.alloc([128, 1], F32)) for _ in range(NZ)]
        b_z = [Buf() for _ in range(NZ)]
        b_zn = [Buf() for _ in range(NZ)]

        def gelu(eng_mul, out, ps_in, tmp, reads, b_tmp, writes):
            self.ACT(out, ps_in, AF.Gelu_apprx_tanh, reads, writes)

        zi = 0
        for g in range(9):
            ctxg = g == 8
            ntile = 2 if ctxg else 4
            N = ntile * 128
            jl = 1 if ctxg else 0
            s = g % 2
            t0 = g * 512
            if g == 0:
                self.load_xg(layer, 1, 0, xg[0], b_xg[0])
                self.DMA("sp", rc[0], self.C["ropec"][:, 0:512], [self.b_const], [b_rt[0]])
                self.DMA("sp", rs[0], self.C["ropes"][:, 0:512], [self.b_const], [b_rt[0]])
            if g + 1 < 9:
                self.load_xg(layer, 1, g + 1, xg[1 - s], b_xg[1 - s])
                if g + 1 < 8:
                    self.DMA("sp", rc[1 - s], self.C["ropec"][:, t0 + 512:t0 + 1024], [self.b_const], [b_rt[1 - s]])
                    self.DMA("sp", rs[1 - s], self.C["ropes"][:, t0 + 512:t0 + 1024], [self.b_const], [b_rt[1 - s]])
            self.make_hT(layer, 1, g, ntile, xg[s], b_xg[s], hT[s], b_hT[s], 3, 4, jl)
            if need_ctx or not ctxg:
                self.DMA("sp", self.hT_d[:, :, t0:t0 + N].rearrange("k p t -> p k t"), hT[s][:, :, 0:N], [b_hT[s]], [self.b_hT[g]])
            for hp in range(5):
                if ctxg and hp < 4 and not need_ctx:
                    continue
                c0 = hp * 128
                pa, pab = self.ps()
                for k in range(8):
                    self.MM(pa[:, 0:N], win[:, k, c0:c0 + 128], hT[s][:, k, 0:N], k == 0, k == 7, [b_win, b_hT[s]], pab)
                if ctxg:
                    self.CP("dve", qk[s][:, hp, 0:N], pa[:, 0:N], [pab], [b_qk[s]])
                    continue
                pp, ppb = self.ps()
                for k in range(8):
                    self.MM(pp[:, 0:N], wperm[:, k, c0:c0 + 128], hT[s][:, k, 0:N], k == 0, k == 7, [b_wperm, b_hT[s]], ppb)
                rr = hp % 2
                self.TT("dve", r1[rr], pa, rc[s], ALU.mult, [pab, b_rt[s]], [b_r[rr]])
                self.TT("dve", r2[rr], pp, rs[s], ALU.mult, [ppb, b_rt[s]], [b_r[rr]])
                self.TT("dve", qk[s][:, hp, :], r1[rr], r2[rr], ALU.add, [b_r[rr]], [b_qk[s]])
            for hp in range(5):
                if ctxg and hp < 4 and not need_ctx:
                    continue
                for hh in range(2):
                    if hp < 4:
                        dst = self.qT_d[2 * hp + hh, :, t0:t0 + N]
                        db = self.b_q[g]
                    else:
                        dst = self.kT_d[hh, :, t0:t0 + N]
                        db = self.b_k[g]
                    self.DMA("sp", dst, qk[s][hh * 64:(hh + 1) * 64, hp, 0:N], [b_qk[s]], [db])
            for tt in range(ntile):
                pvv, pvb = self.ps()
                for k in range(8):
                    self.MM(pvv[:, 0:128], hT[s][:, k, tt * 128:(tt + 1) * 128], win[:, k, 640:768], k == 0, k == 7, [b_win, b_hT[s]], pvb)
                self.CP("act", vt[s][:, tt, :], pvv[:, 0:128], [pvb], [b_vt[s]])
            self.DMA("sp", self.v_d[t0:t0 + N, :].rearrange("(t p) c -> p t c", p=128), vt[s][:, 0:ntile, :], [b_vt[s]], [self.b_v[g]])
            if not ctxg:
                for tt in range(ntile):
                    pf, pfb = self.ps()
                    for k in range(8):
                        self.MM(pf, hT[s][:, k, tt * 128:(tt + 1) * 128], win[:, k, 1792:2304], k == 0, k == 7, [b_win, b_hT[s]], pfb)
                    self.CP("act", ft[s][:, tt, :], pf, [pfb], [b_ft[s]])
                self.DMA("sp", self.f_d[t0:t0 + N, :].rearrange("(t p) c -> p t c", p=128), ft[s], [b_ft[s]], [self.b_f[g]])
            elif need_ctx:
                for cc in range(4):
                    pf, pfb = self.ps()
                    for k in range(8):
                        self.MM(pf[:, 0:N], win[:, k, 1792 + cc * 128:1792 + (cc + 1) * 128], hT[s][:, k, 0:N], k == 0, k == 7, [b_win, b_hT[s]], pfb)
                    self.CP("act", ft[s][:, cc, 0:N], pf[:, 0:N], [pfb], [b_ft[s]])
                self.DMA("sp", self.fcT_d.rearrange("c p t -> p c t"), ft[s][:, :, 0:N], [b_ft[s]], [self.b_fc])
            if ctxg and not need_ctx:
                continue
            for tt in range(ntile):
                zs = tt
                pz, pzb = self.ps()
                for k in range(8):
                    self.MM(pz, hT[s][:, k, tt * 128:(tt + 1) * 128], win[:, k, 1280:1792], k == 0, k == 7, [b_win, b_hT[s]], pzb)
                gelu("dve", zg[zs], pz, zt[zs], [pzb], b_z[zs], [b_z[zs]])
                stats, mv, sd, rstd, nb = zst[zs]
                self.P.op("dve", lambda e, stats=stats, zz=zg[zs]: e.bn_stats(out=stats, in_=zz), [b_z[zs]], [b_z[zs]])
                self.P.op("dve", lambda e, stats=stats, mv=mv: e.bn_aggr(out=mv, in_=stats), [b_z[zs]], [b_z[zs]])
                self.ACT(sd, mv[:, 1:2], AF.Sqrt, [b_z[zs], self.b_pc], [b_z[zs]], bias=self.eps_t[:, 0:1])
                self.P.op("dve", lambda e, rstd=rstd, sd=sd: e.reciprocal(out=rstd, in_=sd), [b_z[zs]], [b_z[zs]])
                self.TS("dve", nb, mv[:, 0:1], -1.0, ALU.mult, [b_z[zs]], [b_z[zs]], s2=rstd[:, 0:1], op1=ALU.mult)
                self.ACT(zt[zs], zg[zs], AF.Identity, [b_z[zs]], [b_z[zs]], scale=rstd[:, 0:1], bias=nb[:, 0:1])
                self.TT("dve", zt[zs], zt[zs], slg, ALU.mult, [b_z[zs], b_sl], [b_z[zs]])
                self.TT("dve", zn[zs], zt[zs], slb, ALU.add, [b_z[zs], b_sl], [b_zn[zs]])
            for cc in range(4):
                pu, pub = self.ps()
                for k in range(8):
                    self.MM(pu[:, 0:N], win[:, k, 768 + cc * 128:768 + (cc + 1) * 128], hT[s][:, k, 0:N], k == 0, k == 7, [b_win, b_hT[s]], pub)
                gs = cc % 2
                gelu("dve", uT[s][:, cc, 0:N], pu[:, 0:N], gtmp[gs][:, 0:N], [pub], b_gt[gs], [b_uT[s]])
            for tt in range(ntile):
                zs = tt
                pm, pmb = self.ps()
                for gq in range(4):
                    self.MM(pm[:, gq * 128:(gq + 1) * 128], zn[zs][:, gq * 128:(gq + 1) * 128], wsT[:, gq, :], True, False, [b_zn[zs], b_wsT], pmb)
                    self.MM(pm[:, gq * 128:(gq + 1) * 128], self.ones_f[0:1, :], sbr[0:1, gq, :], False, True, [self.b_pc, b_sbr], pmb)
                self.TT("dve", sgT[s][:, :, tt * 128:(tt + 1) * 128], pm.rearrange("p (g q) -> p g q", g=4), uT[s][:, :, tt * 128:(tt + 1) * 128], ALU.mult,
                        [pmb, b_uT[s]], [b_sgT[s]])
            self.DMA("sp", self.brT_d[1][:, :, t0:t0 + N].rearrange("c p t -> p c t"), sgT[s][:, :, 0:N], [b_sgT[s]], [self.b_br[1][g]])
        P.barrier()

    def att(self, layer, need_ctx):
        P, A = self.P, self.arena
        A.reset()
        kTa = A.alloc([64, 2, T], BF16)
        b_kTa = Buf()
        self.DMA("sp", kTa, self.kT_d.rearrange("h d t -> d h t"), self.b_k, [b_kTa])
        vall = A.alloc([128, NT, 128], BF16)
        b_vall = Buf()
        self.DMA("sp", vall, self.v_d.rearrange("(t p) c -> p t c", p=128), self.b_v, [b_vall])
        maskp = A.alloc([128, 512], BF16)
        maskn = A.alloc([128, 512], BF16)
        b_mask = Buf()
        self.DMA("sp", maskp, self.C["maskp"], [self.b_const], [b_mask])
        self.DMA("sp", maskn, self.C["maskn"], [self.b_const], [b_mask])
        sk = A.alloc([1, 8], F32)
        b_sk = Buf()
        self.DMA("sp", sk, self.W["attn_sink"][layer:layer + 1, :], [self.b_w], [b_sk])
        self.ACT(sk, sk, AF.Exp, [b_sk], [b_sk])
        skr = A.alloc([1, 8, 128], F32)
        b_skr = Buf()
        self.CP("dve", skr, sk.rearrange("o (h i) -> o h i", i=1).broadcast_to([1, 8, 128]), [b_sk], [b_skr])
        qa = [A.alloc([64, 8, 512], BF16) for _ in range(2)]
        b_qa = [Buf(), Buf()]
        oall = [A.alloc([64, 8, 512], BF16) for _ in range(2)]
        b_oall = [Buf(), Buf()]
        NPT = 12
        pT = [A.alloc([128, 512], BF16) for _ in range(NPT)]
        b_pT = [Buf() for _ in range(NPT)]
        rden = [A.alloc([64, 512], F32) for _ in range(2)]
        b_rden = [Buf(), Buf()]
        st = {"pi": 0, "ri": 0}
        ngroups = 9 if need_ctx else 8

        def load_q(g):
            N = 256 if g == 8 else 512
            t0 = g * 512
            self.DMA("sp", qa[g % 2][:, :, 0:N], self.qT_d[:, :, t0:t0 + N].rearrange("h d t -> d h t"), [self.b_q[g]], [b_qa[g % 2]])

        def pv_stage(pts, kh, s, bl):
            po, pob = self.ps()
            pd, pdb = self.ps()
            n = len(pts)
            for ci, (kt, pp) in enumerate(pts):
                self.MM(po[0:64, :], vall[:, kt, kh * 64:(kh + 1) * 64], pT[pp], ci == 0, ci == n - 1, [b_vall, b_pT[pp]], pob)
                self.MM(pd[0:64, :], self.ones_b, pT[pp], ci == 0, False, [self.b_pc, b_pT[pp]], pdb)
            self.MM(pd[0:64, :], self.ones_f[0:1, 0:64], skr[0:1, kh * 4:(kh + 1) * 4, :], False, True, [self.b_pc, b_skr], pdb)
            rr = st["ri"] % 2
            st["ri"] += 1
            self.P.op("dve", lambda e, o=rden[rr], i_=pd[0:64, :]: e.reciprocal(out=o, in_=i_), [pdb], [b_rden[rr]])
            self.TT("dve", oall[s][:, kh * 4:(kh + 1) * 4, bl * 128:(bl + 1) * 128], po[0:64, :].rearrange("p (g q) -> p g q", g=4),
                    rden[rr].rearrange("p (g q) -> p g q", g=4), ALU.mult, [pob, b_rden[rr]], [b_oall[s]])

        side = self.mods_job(A, layer + 1) if layer + 1 < self.wl else []
        side_i = [0, 0]

        def side_tick():
            side_i[1] += 1
            if side_i[1] % 3 == 0 and side_i[0] < len(side):
                side[side_i[0]]()
                side_i[0] += 1

        load_q(0)
        for g in range(ngroups):
            ctxg = g == 8
            nblk = 2 if ctxg else 4
            N = nblk * 128
            s = g % 2
            t0 = g * 512
            if g + 1 < ngroups:
                load_q(g + 1)
            pending = None
            for bl in range(nblk):
                n = g * 4 + bl
                if ctxg:
                    chunks = [(32, None), (33, None)]
                else:
                    chunks = []
                    if n > 0:
                        chunks.append((n - 1, maskp))
                    chunks.append((n, None))
                    if n < 31:
                        chunks.append((n + 1, maskn))
                    chunks += [(32, None), (33, None)]
                for kh in range(2):
                    rhs_q = qa[s][:, kh * 4:(kh + 1) * 4, bl * 128:(bl + 1) * 128]
                    pts = []
                    for ci, (kt, mask) in enumerate(chunks):
                        pst, pstb = self.ps()
                        self.MM(pst, kTa[:, kh, kt * 128:(kt + 1) * 128], rhs_q, True, mask is None, [b_kTa, b_qa[s]], pstb)
                        if mask is not None:
                            self.MM(pst, self.ident_b, mask, False, True, [self.b_pc, b_mask], pstb)
                        pp = st["pi"] % NPT
                        st["pi"] += 1
                        self.ACT(pT[pp], pst, AF.Exp, [pstb], [b_pT[pp]], scale=0.125)
                        pts.append((kt, pp))
                    if pending is not None:
                        pv_stage(*pending)
                    pending = (pts, kh, s, bl)
                    side_tick()
            pv_stage(*pending)
            self.DMA("sp", self.brT_d[0][:, :, t0:t0 + N].rearrange("h d t -> d h t"), oall[s][:, :, 0:N], [b_oall[s]], [self.b_br[0][g]])
        while side_i[0] < len(side):
            side[side_i[0]]()
            side_i[0] += 1
        P.barrier()

    def fft(self, layer, need_ctx):
        P, A = self.P, self.arena
        A.reset()
        Cn = self.C
        cst = {}
        b_c = Buf()
        for nm in ("c128", "s128n", "s128", "w2"):
            cst[nm] = A.alloc([128, 128], BF16)
            self.DMA("sp", cst[nm], Cn[nm], [self.b_const], [b_c])
        tw128 = A.alloc([128, 32, 2, 128], BF16)
        b_tw = [Buf() for _ in range(4)]
        for q in range(4):
            self.DMA("sp", tw128[:, q * 8:(q + 1) * 8], Cn["tw128"][:, q * 8:(q + 1) * 8], [self.b_const], [b_tw[q]])
        fall = A.alloc([128, 32, 512], BF16)
        b_fall = [Buf() for _ in range(4)]
        fsrc = self.f_d.rearrange("(a b) m -> a b m", b=32)
        for q in range(4):
            self.DMA("sp", fall[:, q * 8:(q + 1) * 8, :], fsrc[:, q * 8:(q + 1) * 8, :], self.b_f, [b_fall[q]])
        Bst = [A.alloc([128, 2, 4, 512], BF16) for _ in range(2)]
        b_Bst = [Buf(), Buf()]
        for n2 in range(32):
            q4 = n2 // 4
            s = q4 % 2
            par, parb = self.ps()
            pai, paib = self.ps()
            self.MM(par, tw128[:, n2, 0, :], fall[:, n2, :], True, True, [b_tw[n2 // 8], b_fall[n2 // 8]], parb)
            self.MM(pai, tw128[:, n2, 1, :], fall[:, n2, :], True, True, [b_tw[n2 // 8], b_fall[n2 // 8]], paib)
            self.ACT(Bst[s][:, 0, n2 % 4, :], par, AF.Copy, [parb], [b_Bst[s]])
            self.CP("dve", Bst[s][:, 1, n2 % 4, :], pai, [paib], [b_Bst[s]])
            if n2 % 4 == 3:
                self.DMA("sp", self.B_d[:, :, q4 * 4:(q4 + 1) * 4, :], Bst[s], [b_Bst[s]], [self.b_Bd[q4]])
        P.barrier()
        A.off = 0
        for nm in ("c128", "s128n", "s128", "w2"):
            A.alloc([128, 128], BF16)
        Bs = A.alloc([128, 64, 512], BF16)
        b_Bs = [Buf() for _ in range(4)]
        bsrc = self.B_d.rearrange("(kp lo) r n c -> (lo r n) kp c", lo=2)
        for q in range(4):
            self.DMA("sp", Bs[:, q * 16:(q + 1) * 16, :], bsrc[:, q * 16:(q + 1) * 16, :], self.b_Bd, [b_Bs[q]])
        XrT = A.alloc([128, 4, SEQ], BF16)
        XiT = A.alloc([128, 4, SEQ], BF16)
        b_X = Buf()
        ev = 0
        for kq in range(16):
            for cc in range(4):
                ps, pb = self.ps()
                for kpl in range(4):
                    kp = kq * 4 + kpl
                    self.MM(ps[:, kpl * 128:(kpl + 1) * 128], Bs[:, kp, cc * 128:(cc + 1) * 128], cst["w2"], True, True, [b_Bs[kp // 16], b_c], pb)
                pv_ = ps.rearrange("p (a r k) -> p a r k", r=2, k=32)
                for rix, XT in ((0, XrT), (1, XiT)):
                    dst = XT[:, cc, :].rearrange("p (k2 k1) -> p k1 k2", k1=128)[:, kq * 8:(kq + 1) * 8, :]
                    eng = "act" if ev % 2 == 0 else "dve"
                    ev += 1
                    if eng == "act":
                        self.ACT(dst, pv_[:, :, rix, :], AF.Copy, [pb], [b_X])
                    else:
                        self.CP("dve", dst, pv_[:, :, rix, :], [pb], [b_X])
        fo = [A.alloc([128, 4, 512], BF16) for _ in range(2)]
        b_fo = [Buf(), Buf()]
        for g in range(8):
            s = g % 2
            for cc in range(4):
                ps, pb = self.ps()
                self.MM(ps, cst["c128"], XrT[:, cc, g * 512:(g + 1) * 512], True, False, [b_c, b_X], pb)
                self.MM(ps, cst["s128"], XiT[:, cc, g * 512:(g + 1) * 512], False, True, [b_c, b_X], pb)
                if cc % 2 == 0:
                    self.ACT(fo[s][:, cc, :], ps, AF.Copy, [pb], [b_fo[s]])
                else:
                    self.CP("dve", fo[s][:, cc, :], ps, [pb], [b_fo[s]])
            self.DMA("sp", self.brT_d[2][:, :, g * 512:(g + 1) * 512].rearrange("c p t -> p c t"), fo[s], [b_fo[s]], [self.b_br[2][g]])
        if need_ctx:
            ccscn = A.alloc([128, 256], BF16)
            c256 = A.alloc([128, 2, 256], BF16)
            s256 = A.alloc([128, 2, 256], BF16)
            b_cc = Buf()
            self.DMA("sp", ccscn, Cn["ccscn"], [self.b_const], [b_cc])
            self.DMA("sp", c256, Cn["c256"], [self.b_const], [b_cc])
            self.DMA("sp", s256, Cn["s256"], [self.b_const], [b_cc])
            fc = A.alloc([128, 4, CTX], BF16)
            b_fcs = Buf()
            self.DMA("sp", fc, self.fcT_d.rearrange("c p t -> p c t"), [self.b_fc], [b_fcs])
            Gs = A.alloc([128, 2, 4, 256], BF16)
            b_Gs = Buf()
            for tt in range(2):
                for cc in range(4):
                    ps, pb = self.ps()
                    self.MM(ps[:, 0:256], fc[:, cc, tt * 128:(tt + 1) * 128], ccscn, True, True, [b_fcs, b_cc], pb)
                    self.CP("dve", Gs[:, tt, cc, :], ps[:, 0:256], [pb], [b_Gs])
            foc = A.alloc([128, 4, CTX], BF16)
            b_foc = Buf()
            for cc in range(4):
                ps, pb = self.ps()
                for tt in range(2):
                    self.MM(ps[:, 0:256], Gs[:, tt, cc, 0:128], c256[:, tt, :], tt == 0, False, [b_Gs, b_cc], pb)
                    self.MM(ps[:, 0:256], Gs[:, tt, cc, 128:256], s256[:, tt, :], False, tt == 1, [b_Gs, b_cc], pb)
                self.CP("dve", foc[:, cc, :], ps[:, 0:256], [pb], [b_foc])
            self.DMA("sp", self.brT_d[2][:, :, SEQ:T].rearrange("c p t -> p c t"), foc, [b_foc], [self.b_br[2][8]])
        P.barrier()

    def merge(self, layer, need_ctx):
        P, A = self.P, self.arena
        A.reset()
        W = self.W
        wg = A.alloc([128, 3, 8, D], BF16)
        b_wg = [Buf() for _ in range(3)]
        for r in range(3):
            self.load_w_cast(wg[:, r], W["w_gate"][layer, r].rearrange("(k p) n -> p k n", p=128), b_wg[r])
        wbr = A.alloc([128, 3, 4, D], BF16)
        b_wb = Buf()
        for r in range(3):
            self.load_w_cast(wbr[:, r], W["w_branch"][layer, r].rearrange("(k p) n -> p k n", p=128), b_wb)
        att4 = self.brT_d[0].rearrange("(c two) d t -> c (two d) t", two=2)
        wo = A.alloc([128, 8, D], BF16)
        b_wo = Buf()
        self.load_w_cast(wo, W["w_out"][layer].rearrange("(k p) n -> p k n", p=128), b_wo)
        P.gate_swdge()
        G, b_G, Gc, b_Gc, lng, lnb, b_ln = self.load_epi_consts(A, layer, 5, 1, False)
        hT = [A.alloc([128, 8, 512], BF16) for _ in range(2)]
        b_hT = [Buf(), Buf()]
        at = [A.alloc([128, 4, 512], BF16) for _ in range(2)]
        sg = [A.alloc([128, 4, 512], BF16) for _ in range(2)]
        fo = [A.alloc([128, 4, 512], BF16) for _ in range(2)]
        b_br = [[Buf(), Buf()] for _ in range(3)]
        mT = A.alloc([128, 8, 512], BF16)
        b_mT = Buf()
        sig = [A.alloc([128, 512], F32) for _ in range(3)]
        b_sig = [Buf() for _ in range(3)]
        acc = [A.alloc([128, 512], F32) for _ in range(2)]
        b_acc = [Buf(), Buf()]
        tmp = [A.alloc([128, 512], F32) for _ in range(2)]
        b_tmp = [Buf(), Buf()]
        xt = [A.alloc([128, D], F32) for _ in range(4)]
        b_xt = [Buf() for _ in range(4)]
        work = self.alloc_epi_work(A, 2)
        it = 0
        si = 0
        ngroups = 9 if need_ctx else 8
        ntiles_total = sum(4 if g < 8 else 2 for g in range(ngroups))

        def load_grp(g):
            ntile = 4 if g < 8 else 2
            N = ntile * 128
            s = g % 2
            t0 = g * 512
            self.DMA("sp", hT[s][:, :, 0:N], self.hT_d[:, :, t0:t0 + N].rearrange("k p t -> p k t"), [self.b_hT[g]], [b_hT[s]])
            self.DMA("sp", at[s][:, :, 0:N], att4[:, :, t0:t0 + N].rearrange("c p t -> p c t"), [self.b_br[0][g]], [b_br[0][s]])
            self.DMA("sp", sg[s][:, :, 0:N], self.brT_d[1][:, :, t0:t0 + N].rearrange("c p t -> p c t"), [self.b_br[1][g]], [b_br[1][s]])
            self.DMA("sp", fo[s][:, :, 0:N], self.brT_d[2][:, :, t0:t0 + N].rearrange("c p t -> p c t"), [self.b_br[2][g]], [b_br[2][s]])

        def load_x(t):
            src, sb_ = self.xsrc(layer, 1, t)
            self.DMA("sp", xt[t % 4], src, [sb_], [b_xt[t % 4]])

        load_grp(0)
        load_x(0)
        load_x(1)
        for g in range(ngroups):
            ntile = 4 if g < 8 else 2
            N = ntile * 128
            s = g % 2
            t0 = g * 512
            if g + 1 < ngroups:
                load_grp(g + 1)
            for fc in range(8):
                a = fc % 2
                for r in range(3):
                    pg, pgb = self.ps()
                    for k in range(8):
                        self.MM(pg[:, 0:N], wg[:, r, k, fc * 128:(fc + 1) * 128], hT[s][:, k, 0:N], k == 0, k == 7, [b_wg[r], b_hT[s]], pgb)
                    pbr, pbb = self.ps()
                    src = (at, sg, fo)[r]
                    for k in range(4):
                        self.MM(pbr[:, 0:N], wbr[:, r, k, fc * 128:(fc + 1) * 128], src[s][:, k, 0:N], k == 0, k == 3, [b_wb, b_br[r][s]], pbb)
                    ss = si % 3
                    si += 1
                    self.ACT(sig[ss][:, 0:N], pg[:, 0:N], AF.Sigmoid, [pgb], [b_sig[ss]])
                    if r == 0:
                        self.TT("dve", acc[a][:, 0:N], sig[ss][:, 0:N], pbr[:, 0:N], ALU.mult, [b_sig[ss], pbb], [b_acc[a]])
                    else:
                        x_ = (r - 1)
                        self.TT("dve", tmp[x_][:, 0:N], sig[ss][:, 0:N], pbr[:, 0:N], ALU.mult, [b_sig[ss], pbb], [b_tmp[x_]])
                        if r == 1:
                            self.TT("dve", acc[a][:, 0:N], acc[a][:, 0:N], tmp[x_][:, 0:N], ALU.add, [b_acc[a], b_tmp[x_]], [b_acc[a]])
                        else:
                            self.TT("dve", mT[:, fc, 0:N], acc[a][:, 0:N], tmp[x_][:, 0:N], ALU.add, [b_acc[a], b_tmp[x_]], [b_mT])
            for pr in range(ntile // 2):
                tiles = []
                for tt in (2 * pr, 2 * pr + 1):
                    t = g * 4 + tt
                    xs = t % 4
                    ws = tt % 2
                    if t + 2 < ntiles_total:
                        load_x(t + 2)
                    halves = []
                    for h in range(2):
                        ps, pb = self.ps()
                        for k in range(8):
                            self.MM(ps, mT[:, k, tt * 128:(tt + 1) * 128], wo[:, k, h * 512:(h + 1) * 512], k == 0, k == 7, [b_mT, b_wo], pb)
                        halves.append((ps, pb))
                    dst, db = self.xdst(layer, 1, t, False)
                    tiles.append(dict(halves=halves, xt=xt[xs], b_xt=b_xt[xs], G=G if g < 8 else Gc, b_G=b_G if g < 8 else b_Gc,
                                      work=work[ws], dst=dst, b_dst=db))
                self.epilogue_multi(tiles, lng, lnb, b_ln)
        P.barrier()

    def build(self):
        phases = []
        phases.append(("prologue", self.prologue))
        for i in range(self.L):
            last = i == self.L - 1
            need_ctx = not last
            phases.append(("L%d_ffn0a" % i, lambda i=i: self.ffn_a(i, 0)))
            phases.append(("L%d_ffn0b" % i, lambda i=i: self.ffn_b(i, 0)))
            phases.append(("L%d_mixp" % i, lambda i=i, nc_=need_ctx: self.mix_p(i, nc_)))
            phases.append(("L%d_att" % i, lambda i=i, nc_=need_ctx: self.att(i, nc_)))
            phases.append(("L%d_fft" % i, lambda i=i, nc_=need_ctx: self.fft(i, nc_)))
            phases.append(("L%d_merge" % i, lambda i=i, nc_=need_ctx: self.merge(i, nc_)))
            phases.append(("L%d_ffn1a" % i, lambda i=i: self.ffn_a(i, 1)))
            phases.append(("L%d_ffn1b" % i, lambda i=i, last=last: self.ffn_b(i, 1, ngroups=8 if last else 9, final=last)))
        for name, fn in phases:
            fn()
            if self.stop_after == name:
                break
        self.debug_dump()
        self.P.final_wait("sp")
        self.P.emit()
        return self.nc

    def debug_dump(self):
        if not self.dbg:
            return
        A = self.arena
        A.reset()
        for name, (ap, getter) in self.dbg.items():
            src = getter(self)
            R, Ccols = src.shape
            dt_ = src.dtype
            cw = 4096
            bufs = [A.alloc([128, cw], dt_) for _ in range(2)]
            bb = [Buf(), Buf()]
            it = 0
            for r0 in range(0, R, 128):
                rr = min(128, R - r0)
                for c0 in range(0, Ccols, cw):
                    cc = min(cw, Ccols - c0)
                    s_ = it % 2
                    it += 1
                    self.DMA("sp", bufs[s_][0:rr, 0:cc], src[r0:r0 + rr, c0:c0 + cc], [], [bb[s_]])
                    self.DMA("sp", ap[r0:r0 + rr, c0:c0 + cc], bufs[s_][0:rr, 0:cc], [bb[s_]], [Buf()])


_CACHE = {}


def _get_program():
    if "nc" not in _CACHE:
        _CACHE["nc"] = Builder(DEPTH).build()
        _CACHE["consts"] = make_consts()
    return _CACHE["nc"], _CACHE["consts"]


def kernel(x, c, ctx, c_ctx, w_mod, b_mod, w_ffn_up, w_ffn_down, ln_g, ln_b, w_in, attn_sink,
           sgu_w, sgu_b, sgu_ln_g, sgu_ln_b, w_gate, w_branch, w_out):
    nc, consts = _get_program()
    f32 = lambda a: np.ascontiguousarray(np.asarray(a, dtype=np.float32))
    shared = {"c_ctx": f32(c_ctx).reshape(1, D), "w_mod": f32(w_mod), "b_mod": f32(b_mod), "w_ffn_up": f32(w_ffn_up),
              "w_ffn_down": f32(w_ffn_down), "ln_g": f32(ln_g), "ln_b": f32(ln_b), "w_in": f32(w_in), "attn_sink": f32(attn_sink),
              "sgu_w": f32(sgu_w), "sgu_b": f32(sgu_b), "sgu_ln_g": f32(sgu_ln_g), "sgu_ln_b": f32(sgu_ln_b),
              "w_gate": f32(w_gate), "w_branch": f32(w_branch), "w_out": f32(w_out)}
    shared.update(consts)
    x = f32(x)
    c = f32(c)
    ctx = f32(ctx)
    in_maps = []
    for b in range(8):
        m = dict(shared)
        m["x"] = x[b]
        m["c"] = c[b].reshape(1, D)
        m["ctx"] = ctx[b]
        in_maps.append(m)
    res = run_bass_kernel_spmd(nc, in_maps, core_ids=list(range(8)))
    return np.stack([np.asarray(r["out"], dtype=np.float32) for r in res.results], axis=0)
```
